# Optimizing a Trainium2 kernel written in Bass

```python
import jax, jax.numpy as jnp
from jax import lax
import numpy as np

D_MODEL = 1024
BATCH = 32
SEQ = 256
DEPTH = 4
DEC_BATCH = 4
DEC_SEQ = 4096
PAST_LEN = 256

GRID_W = 64
MIX_W = D_MODEL
BRANCH_W = MIX_W // 2
H_A = 4
DK_A = BRANCH_W // (2 * H_A)
DV_A = BRANCH_W // H_A
GATE_RANK = 16
GATE_NORMALIZER = 16.0
CHUNK_A = 64
H_B = 4
DH_B = BRANCH_W // H_B
CHUNK_B = 128
POOL_WINDOWS = (2, 4, 8, 16)
G_C = 4
DG_C = BRANCH_W // G_C
CONV_W = 3
N_EVEN = (DEPTH + 1) // 2
N_ODD = DEPTH // 2
EVEN_SPLITS = (H_A * DK_A, H_A * DK_A, BRANCH_W, BRANCH_W, GATE_RANK, GATE_RANK,
               BRANCH_W, BRANCH_W, BRANCH_W)
EVEN_IN = sum(EVEN_SPLITS)
ODD_SPLITS = (BRANCH_W,) * 6
ODD_IN = sum(ODD_SPLITS)
EPS = 1e-6

kernel_name = 'hybrid_gla_sgu_pool_conv_diffusion_step'


def split_cols(z, sizes):
    idx = np.cumsum(sizes)[:-1].tolist()
    return jnp.split(z, idx, axis=-1)


def rmsnorm(x, g):
    xf = x.astype(jnp.float32)
    y = xf * lax.rsqrt(jnp.mean(xf * xf, axis=-1, keepdims=True) + EPS)
    return (y * g.astype(jnp.float32)).astype(x.dtype)


def adaln(cond, w, b):
    m = jax.nn.silu(cond) @ w + b
    shift, scale, gate = jnp.split(m, 3, axis=-1)
    return shift[:, None], scale[:, None], gate[:, None]


def to_col_major(x, rows):
    B, T, C = x.shape
    return x.reshape(B, rows, GRID_W, C).transpose(0, 2, 1, 3).reshape(B, T, C)


def from_col_major(x, rows):
    B, T, C = x.shape
    return x.reshape(B, GRID_W, rows, C).transpose(0, 2, 1, 3).reshape(B, T, C)


def gla_scan(q, k, v, g, s0):
    dt = v.dtype
    B, T, H, DK = q.shape
    DV = v.shape[-1]
    n = T // CHUNK_A

    def chunks(a):
        return a.astype(jnp.float32).reshape(B, n, CHUNK_A, H, a.shape[-1]).swapaxes(0, 1)

    tri = jnp.tril(jnp.ones((CHUNK_A, CHUNK_A), bool))[None, :, :, None, None]

    def step(S, inp):
        qc, kc, vc, gc = inp
        b = jnp.cumsum(gc, axis=1)
        o_inter = jnp.einsum('bihk,bhkv->bihv', qc * jnp.exp(b), S)
        diff = b[:, :, None] - b[:, None]
        decay = jnp.where(tri, jnp.exp(jnp.where(tri, diff, 0.0)), 0.0)
        att = jnp.einsum('bihk,bjhk,bijhk->bhij', qc, kc, decay)
        o_intra = jnp.einsum('bhij,bjhv->bihv', att, vc)
        b_last = b[:, -1]
        S = jnp.exp(b_last)[..., None] * S + jnp.einsum(
            'bjhk,bjhv->bhkv', kc * jnp.exp(b_last[:, None] - b), vc)
        return S, o_inter + o_intra

    S, o = lax.scan(step, s0.astype(jnp.float32), (chunks(q), chunks(k), chunks(v), chunks(g)))
    o = o.swapaxes(0, 1).reshape(B, T, H, DV)
    return o.astype(dt), S.astype(dt)


def centred_mean(x, w):
    B, T, C = x.shape
    cs = jnp.concatenate([jnp.zeros((B, 1, C), jnp.float32),
                          jnp.cumsum(x.astype(jnp.float32), axis=1)], axis=1)
    t = jnp.arange(T)
    lo = jnp.clip(t - w // 2, 0, T)
    hi = jnp.clip(t + w - w // 2, 0, T)
    s = jnp.take(cs, hi, axis=1) - jnp.take(cs, lo, axis=1)
    cnt = (hi - lo).astype(jnp.float32)
    return (s / cnt[None, :, None]).astype(x.dtype)


def setup_inputs(seed: int = 0) -> dict:
    key = jax.random.key(seed)
    ks = jax.random.split(key, 24)
    f32 = jnp.float32
    nrm = lambda k, shape, s: jax.random.normal(k, shape, f32) * s
    return {
        'x_prompt': nrm(ks[0], (BATCH, SEQ, D_MODEL), 1.0),
        'x_sample': nrm(ks[1], (DEC_BATCH, DEC_SEQ, D_MODEL), 1.0),
        'c': nrm(ks[2], (DEC_BATCH, D_MODEL), 1.0),
        'state_gla': nrm(ks[3], (DEC_BATCH, N_EVEN, 2, H_A, DK_A, DV_A), 1.0),
        'c_ctx': nrm(ks[4], (D_MODEL,), 1.0),
        'w_ada': nrm(ks[5], (DEPTH, D_MODEL, 3 * D_MODEL), 0.2 * D_MODEL ** -0.5),
        'b_ada': nrm(ks[6], (DEPTH, 3 * D_MODEL), 0.02),
        'norm_g': 1.0 + nrm(ks[7], (DEPTH, D_MODEL), 0.02),
        'w_in_even': nrm(ks[8], (N_EVEN, D_MODEL, EVEN_IN), D_MODEL ** -0.5),
        'w_in_odd': nrm(ks[9], (N_ODD, D_MODEL, ODD_IN), D_MODEL ** -0.5),
        'w_out': nrm(ks[10], (DEPTH, MIX_W, D_MODEL), 0.5 * MIX_W ** -0.5),
        'w_gk': nrm(ks[11], (N_EVEN, 2, GATE_RANK, H_A * DK_A), GATE_RANK ** -0.5),
        'b_gk': nrm(ks[12], (N_EVEN, 2, H_A * DK_A), 0.1),
        'gla_norm_g': 1.0 + nrm(ks[13], (N_EVEN, DV_A), 0.02),
        'sgu_norm_g': 1.0 + nrm(ks[14], (N_EVEN, BRANCH_W), 0.02),
        'w_s': nrm(ks[15], (N_EVEN, H_B, CHUNK_B, CHUNK_B), CHUNK_B ** -0.5),
        'b_s': 1.0 + nrm(ks[16], (N_EVEN, H_B, CHUNK_B), 0.1),
        'w_pool': nrm(ks[17], (N_ODD, G_C, DG_C, DG_C), DG_C ** -0.5),
        'pool_scale': 1.0 + nrm(ks[18], (N_ODD, BRANCH_W), 0.1),
        'w_conv': nrm(ks[19], (N_ODD, CONV_W, BRANCH_W), CONV_W ** -0.5),
        'final_norm_g': 1.0 + nrm(ks[20], (D_MODEL,), 0.02),
    }


def reference(x_prompt, x_sample, c, state_gla, c_ctx, w_ada, b_ada, norm_g, w_in_even,
              w_in_odd, w_out, w_gk, b_gk, gla_norm_g, sgu_norm_g, w_s, b_s, w_pool,
              pool_scale, w_conv, final_norm_g):

    def mix_even(z, j, s0_f, s0_b):
        B, T, _ = z.shape
        q, k, v, ga, lr_f, lr_b, u, vs, gb = split_cols(z, EVEN_SPLITS)
        q = q.reshape(B, T, H_A, DK_A) * (DK_A ** -0.5)
        k = k.reshape(B, T, H_A, DK_A)
        v = v.reshape(B, T, H_A, DV_A)

        def log_decay(lr, d):
            gl = (lr @ w_gk[j, d] + b_gk[j, d]).astype(jnp.float32)
            return (jax.nn.log_sigmoid(gl) / GATE_NORMALIZER).reshape(B, T, H_A, DK_A)

        flip = lambda a: a[:, ::-1]
        o_f, s_f = gla_scan(q, k, v, log_decay(lr_f, 0), s0_f)
        o_r, s_b = gla_scan(flip(q), flip(k), flip(v), flip(log_decay(lr_b, 1)), s0_b)
        o = o_f + flip(o_r)
        o_a = rmsnorm(o, gla_norm_g[j]).reshape(B, T, BRANCH_W) * jax.nn.silu(ga)
        n = T // CHUNK_B
        vs = rmsnorm(vs, sgu_norm_g[j]).reshape(B, n, CHUNK_B, H_B, DH_B)
        sp = jnp.einsum('hij,bnjhd->bnihd', w_s[j], vs) + b_s[j].T[None, None, :, :, None]
        o_b = u * sp.reshape(B, T, BRANCH_W) * jax.nn.silu(gb)
        return jnp.concatenate([o_a, o_b], axis=-1), s_f, s_b

    def mix_odd(z, j):
        B, T, _ = z.shape
        xc, gc, xd, bd, cd, gd = split_cols(z, ODD_SPLITS)
        pooled = jnp.concatenate(
            [centred_mean(xc[..., i * DG_C:(i + 1) * DG_C], w) for i, w in enumerate(POOL_WINDOWS)],
            axis=-1) - xc
        pooled = jnp.einsum('btgc,gcd->btgd', pooled.reshape(B, T, G_C, DG_C), w_pool[j])
        o_c = pooled.reshape(B, T, BRANCH_W) * pool_scale[j] * jax.nn.silu(gc)
        u = cd * xd
        up = jnp.pad(u, ((0, 0), (1, 1), (0, 0)))
        y = up[:, :-2] * w_conv[j, 0] + up[:, 1:-1] * w_conv[j, 1] + up[:, 2:] * w_conv[j, 2]
        o_d = bd * y * jax.nn.silu(gd)
        return jnp.concatenate([o_c, o_d], axis=-1)

    def run_stream(x, cond, gla_init, rows, latent):
        finals = []
        for l in range(DEPTH):
            shift, scale, gate = adaln(cond, w_ada[l], b_ada[l])
            h = rmsnorm(x, norm_g[l]) * (1.0 + scale) + shift
            col = latent and (l // 2) % 2 == 1
            if col:
                h = to_col_major(h, rows)
            j = l // 2
            if l % 2 == 0:
                out, s_f, s_b = mix_even(h @ w_in_even[j], j, gla_init[j][0], gla_init[j][1])
                finals.append(jnp.stack([s_f, s_b], axis=1))
            else:
                out = mix_odd(h @ w_in_odd[j], j)
            out = out @ w_out[l]
            if col:
                out = from_col_major(out, rows)
            x = x + gate * out
        return rmsnorm(x, final_norm_g), finals

    Bp = x_prompt.shape[0]
    zero_s = jnp.zeros((Bp, H_A, DK_A, DV_A), x_prompt.dtype)
    ctx_init = [(zero_s, zero_s) for _ in range(N_EVEN)]
    y_prompt, finals = run_stream(x_prompt, c_ctx[None], ctx_init, 0, False)
    new_state_gla = jnp.stack(finals, axis=1)

    rows = x_sample.shape[1] // GRID_W
    lat_init = [(state_gla[:, j, 0], state_gla[:, j, 1]) for j in range(N_EVEN)]
    y_sample, _ = run_stream(x_sample, c, lat_init, rows, True)

    return (y_prompt, y_sample, new_state_gla)
```

```python
import contextlib
import math
import numpy as np
import concourse.bass as bass
import concourse.mybir as mybir
from concourse.bass_utils import run_bass_kernel_spmd

F32 = mybir.dt.float32
BF16 = mybir.dt.bfloat16
AF = mybir.ActivationFunctionType
ALU = mybir.AluOpType

D = 1024
P = 128
NPROMPT = 4
TP = 256
TS = 4096
EVEN_IN = 3104
ODD_IN = 3072
Q0, K0, V0, GA0, LRF0, LRB0, U0, VS0, GB0 = 0, 256, 512, 1024, 1536, 1552, 1568, 2080, 2592
XC0, GC0, XD0, BD0, CD0, GD0 = 0, 512, 1024, 1536, 2048, 2560
EPS = 1e-6


class Buf:
    __slots__ = ("name", "lw", "rd")

    def __init__(self, name):
        self.name = name
        self.lw = None
        self.rd = {}


class Sched:
    ENGS = ("sp", "act", "dve", "pool", "pe")
    K = 8

    def __init__(self):
        self.ops = {e: [] for e in self.ENGS}
        self.cnt = {e: 0 for e in self.ENGS}
        self.waited = {e: {} for e in self.ENGS}
        self.dq = {e: {"n": 0, "tgt": [0] * self.K, "last": [None] * self.K} for e in self.ENGS}
        self.semkeys = set()

    def _deps(self, reads, writes):
        deps = []
        for b in reads:
            if b.lw is not None:
                deps.append(b.lw)
        for b in writes:
            if b.lw is not None:
                deps.append(b.lw)
            deps.extend(b.rd.items())
        return deps

    def _filter(self, eng, deps):
        w = self.waited[eng]
        out = {}
        for k, v in deps:
            if w.get(k, 0) >= v:
                continue
            if out.get(k, 0) < v:
                out[k] = v
        for k, v in out.items():
            w[k] = v
        return list(out.items())

    def _mark(self, ident, reads, writes):
        k, v = ident
        for b in reads:
            if b.rd.get(k, 0) < v:
                b.rd[k] = v
        for b in writes:
            b.lw = ident
            b.rd = {}

    def op(self, eng, fn, reads=(), writes=()):
        deps = self._deps(reads, writes)
        if eng == "pe":
            deps = [d for d in deps if d[0] != ("c", "pe")]
        waits = self._filter(eng, deps)
        self.cnt[eng] += 1
        key = ("c", eng)
        self.semkeys.add(key)
        self.ops[eng].append((waits, fn, key, 1))
        self._mark((key, self.cnt[eng]), reads, writes)

    def dma(self, q, fn, reads=(), writes=(), n=1):
        d = self.dq[q]
        slot = d["n"] % self.K
        d["n"] += 1
        deps = self._deps(reads, writes)
        if d["last"][slot] is not None:
            deps.append(d["last"][slot])
        waits = self._filter(q, deps)
        d["tgt"][slot] += 16 * n
        key = ("d", q, slot)
        self.semkeys.add(key)
        ident = (key, d["tgt"][slot])
        d["last"][slot] = ident
        self.ops[q].append((waits, fn, key, 16))
        self._mark(ident, reads, writes)

    def finish(self, eng="sp"):
        deps = []
        for q in self.ENGS:
            for ident in self.dq[q]["last"]:
                if ident is not None:
                    deps.append(ident)
        waits = self._filter(eng, deps)
        self.ops[eng].append((waits, None, None, 0))

    def emit(self, nc, stack):
        sems = {}
        for k in sorted(self.semkeys, key=str):
            sems[k] = stack.enter_context(nc.semaphore("s_" + "_".join(str(x) for x in k)))
        ops = self.ops

        def replay(name, e):
            for waits, fn, key, amt in ops[name]:
                for k, v in waits:
                    e.wait_ge(sems[k], v)
                if fn is None:
                    continue
                ins = fn(e)
                if isinstance(ins, (list, tuple)):
                    for i in ins:
                        i.then_inc(sems[key], amt)
                else:
                    ins.then_inc(sems[key], amt)

        with nc.Block() as block:
            @block.sync
            def _(e):
                replay("sp", e)

            @block.scalar
            def _(e):
                replay("act", e)

            @block.vector
            def _(e):
                replay("dve", e)

            @block.gpsimd
            def _(e):
                replay("pool", e)

            @block.tensor
            def _(e):
                replay("pe", e)


class Ring:
    def __init__(self, tiles):
        self.tiles = tiles
        self.bufs = [Buf("r") for _ in tiles]
        self.i = 0

    def next(self):
        k = self.i % len(self.tiles)
        self.i += 1
        return self.tiles[k], self.bufs[k]


def build_nc(depth=4, usel=None):
    nc = bass.Bass("TRN2", target_bir_lowering=False)
    S = Sched()

    def din(name, shape):
        return nc.dram_tensor(name, list(shape), F32, kind="ExternalInput").ap()

    xp_d = din("xp", [NPROMPT * TP, D])
    xs_d = din("xs", [TS, D])
    cond_d = din("cond", [16, P])
    st0_d = din("st0", [2, 2, 4, 64, 128])
    wada_d = din("w_ada", [4, D, 3 * D])
    bada_d = din("b_ada", [4, 3 * D])
    ng_d = din("norm_g", [4, D])
    wie_d = din("w_in_even", [2, D, EVEN_IN])
    wio_d = din("w_in_odd", [2, D, ODD_IN])
    wout_d = din("w_out", [4, D, D])
    wgk_d = din("w_gk", [2, 2, 16, 256])
    bgk_d = din("b_gk", [2, 2, 256])
    gng_d = din("gla_norm_g", [2, 128])
    sng_d = din("sgu_norm_g", [2, 512])
    ws_d = din("w_s", [2, 4, 128, 128])
    bs_d = din("b_s", [2, 4, 128])
    wpool_d = din("w_pool", [2, 4, 128, 128])
    pscale_d = din("pool_scale", [2, 512])
    wconv_d = din("w_conv", [2, 3, 512])
    fng_d = din("final_norm_g", [D])
    ident_d = din("c_ident", [P, P])
    tri_d = din("c_tri", [2, P, P])
    mask_d = din("c_mask", [2, P, P])
    rc_d = din("c_rc", [2 * 4 * 128])
    sel_d = din("c_sel", [2, 2, P])
    ones_d = din("c_ones", [1, P])

    yp_d = nc.dram_tensor("yp", [NPROMPT * TP, D], F32, kind="ExternalOutput").ap()
    ys_d = nc.dram_tensor("ys", [TS, D], F32, kind="ExternalOutput").ap()
    ns_d = nc.dram_tensor("ns", [NPROMPT, 2, 2, 4, 64, 128], F32, kind="ExternalOutput").ap()

    NTOK = NPROMPT * TP + TS
    xscr = [nc.dram_tensor(f"xscr{i}", [NTOK, D], F32, kind="Internal").ap() for i in range(2)]
    oscr = nc.dram_tensor("oscr", [512, NTOK], F32, kind="Internal").ap()

    units = [(u * TP, TP, 0, False) for u in range(NPROMPT)] + [(NPROMPT * TP, TS, 1, True)]
    bxs = [[Buf(f'xscr{i}_{u}') for u in range(len(units))] for i in range(2)]
    bos = [Buf(f'oscr_{u}') for u in range(len(units))]

    with contextlib.ExitStack() as st:
        def sb(name, shape, dt=F32):
            return st.enter_context(nc.sbuf_tensor(name, list(shape), dt))

        def ps(name, shape, dt=F32):
            return st.enter_context(nc.psum_tensor(name, list(shape), dt))

        def ring(name, shape, dt, n):
            return Ring([sb(f"{name}{i}", shape, dt) for i in range(n)])

        win = sb("win", [P, 8, EVEN_IN], BF16)
        bwin = [Buf(f"win{k}") for k in range(8)]
        wout = sb("wout", [P, 8, D], BF16)
        bwout = [Buf(f"wout{k}") for k in range(8)]
        wada = sb("wada", [P, 8, 512], BF16)
        bwada = Buf("wada")

        xt_r = ring("xt", [P, D], F32, 5)
        xn_r = ring("xn", [P, D], BF16, 1)
        hT_r = ring("hT", [P, 8, P], BF16, 3)
        t4_r = ring("t4", [P, D], F32, 2)
        mix_r = ring("mixT", [P, 8, P], BF16, 2)
        f2k = ring("f2k", [P, 512], F32, 8)
        hold_r = ring("hold", [P, 512], F32, 6)
        sm_r = ring("sm", [P, 4], F32, 8)

        psr = Ring([ps(f"psr{i}", [P, 512], F32) for i in range(7)])
        tps = ps("tps", [P, 1024], BF16)
        btps = Buf("tps")

        identf = sb("identf", [P, P]); bident = Buf("identf")
        identb = sb("identb", [P, P], BF16); bidentb = Buf("identb")
        tri = sb("tri", [P, 2, P]); btri = Buf("tri")
        mask = sb("mask", [P, 2, P]); bmask = Buf("mask")
        rc = sb("rc", [P, 2, 4, P]); brc = Buf("rc")
        sel = sb("sel", [2, 2, P]); bsel = Buf("sel")
        ones128 = sb("ones128", [P, P], BF16); bones = Buf("ones128")
        osqb_r = ring("osqb", [P, 512], BF16, 1)
        fgb = sb("fgb", [P, D]); bfgb = Buf("fgb")
        neghalf = sb("neghalf", [P, 1]); bneghalf = Buf("neghalf")
        stA = sb("stA", [96, P]); bstA = Buf("stA")
        stB = sb("stB", [82, P]); bstB = Buf("stB")
        colA = sb("colA", [P, 96]); bcolA = Buf("colA")
        colB = sb("colB", [P, 82]); bcolB = Buf("colB")
        scT = sb("scT", [P, 8, 2], BF16); bscT = Buf("scT")
        adaC = sb("adaC", [P, 16, 2]); badaC = Buf("adaC")
        gsC = sb("gsC", [P, 8, 2]); bgsC = Buf("gsC")
        gateb = [sb(f"gateb{c}", [P, D]) for c in range(2)]
        bgateb = [Buf(f"gateb{c}") for c in range(2)]
        wsT = sb("wsT", [P, 2, 4, P], BF16); bwsT = Buf("wsT")
        wpool = sb("wpool", [P, 2, 4, P], BF16); bwpool = Buf("wpool")
        wgk = sb("wgk", [17, 4, 256]); bwgk = Buf("wgk")
        sgugb = sb("sgugb", [P, 512]); bsgugb = Buf("sgugb")
        bsb = sb("bsb", [P, 512]); bbsb = Buf("bsb")

        lrT_r = ring("lrT", [17, P], F32, 1)
        eg_r = ring("eg", [P, 256], F32, 1)
        Eq_r = ring("Eq", [P, 256], F32, 1)
        Ek_r = ring("Ek", [P, 256], F32, 1)
        qd_r = ring("qdT", [P, 256], BF16, 2)
        kd_r = ring("kdT", [P, 256], BF16, 1)
        kdt_r = ring("kdtok", [P, 256], BF16, 2)
        vt_r = ring("vtok", [P, 512], BF16, 3)
        att_r = ring("attT", [P, 4, P], BF16, 2)
        tS_r = ring("tS", [P, 2, P], F32, 1)
        Sst = [sb(f"Sst{d}", [P, 2, P]) for d in range(2)]
        bSst = [Buf(f"Sst{d}") for d in range(2)]
        Sbf = [sb(f"Sbf{d}", [P, 2, P], BF16) for d in range(2)]
        bSbf = [Buf(f"Sbf{d}") for d in range(2)]
        vsn_r = ring("vsn", [P, 512], BF16, 1)
        XC = [sb(f"XC{i}", [P, 4, 144]) for i in range(2)]
        bXC = [Buf(f"XC{i}") for i in range(2)]
        UP = [sb(f"UP{i}", [P, 4, 130]) for i in range(2)]
        bUP = [Buf(f"UP{i}") for i in range(2)]
        L1 = sb("L1", [P, 4, 143]); bL1 = Buf("L1")
        L2 = sb("L2", [P, 3, 141]); bL2 = Buf("L2")
        L3 = sb("L3", [P, 2, 137]); bL3 = Buf("L3")
        L4 = sb("L4", [P, P]); bL4 = Buf("L4")
        pooled_r = ring("pooled", [P, 4, P], BF16, 1)

        def mm_group(pst, bps, items, reads):
            def fn(e, items=items):
                last = None
                for out_ap, pairs in items:
                    n = len(pairs)
                    for i, (l, r) in enumerate(pairs):
                        last = e.matmul(out_ap, lhsT=l, rhs=r, start=(i == 0), stop=(i == n - 1))
                return last
            S.op("pe", fn, reads=reads, writes=[bps])

        def v4(ap):
            return ap.rearrange("p (a b) -> p a b", a=4)

        def dma_in(out, in_, writes, q="sp", reads=()):
            S.dma(q, lambda e: e.dma_start(out=out, in_=in_), reads=reads, writes=writes)

        dma_in(identf[:], ident_d, [bident])
        dma_in(tri[:], tri_d.rearrange("d j i -> j d i"), [btri])
        dma_in(mask[:], mask_d.rearrange("d j i -> j d i"), [bmask])
        dma_in(rc[:].rearrange("p a g n -> p (a g n)"), rc_d.partition_broadcast(P), [brc])
        dma_in(sel[:], sel_d.rearrange("c r n -> r c n"), [bsel])
        dma_in(fgb[:], fng_d.partition_broadcast(P), [bfgb])
        dma_in(stA[:], bada_d.rearrange("l (c p) -> (l c) p", p=P), [bstA])
        dma_in(stB[0:16, :], cond_d, [bstB])
        dma_in(stB[16:48, :], ng_d.rearrange("l (c p) -> (l c) p", p=P), [bstB])
        dma_in(stB[48:50, :], gng_d, [bstB])
        dma_in(stB[50:58, :], pscale_d.rearrange("j (c p) -> (j c) p", p=P), [bstB])
        dma_in(stB[58:82, :], wconv_d.rearrange("j k (c p) -> (j k c) p", p=P), [bstB])
        dma_in(wgk[0:16, :, :], wgk_d.rearrange("j d r n -> r (j d) n"), [bwgk])
        dma_in(wgk[16:17, :, :], bgk_d.rearrange("(o j) d n -> o (j d) n", o=1), [bwgk])
        for _t, _b in zip(lrT_r.tiles, lrT_r.bufs):
            dma_in(_t[16:17, :], ones_d, [_b])
        _wsr, bwsraw = t4_r.next()
        wsraw = _wsr[:].rearrange("p (a b) -> p a b", a=8)
        dma_in(wsraw, ws_d.rearrange("j h i k -> i (j h) k"), [bwsraw])
        dma_in(wpool[:].rearrange("p j g n -> p (j g) n"), wpool_d.rearrange("j g c n -> c (j g) n"), [bwpool], q="pool")
        S.op("dve", lambda e: e.tensor_copy(out=identb[:], in_=identf[:]), reads=[bident], writes=[bidentb])
        S.op("pool", lambda e: e.memset(ones128[:], 1.0 / 128.0), writes=[bones])
        S.op("pool", lambda e: e.memset(neghalf[:], -0.5), writes=[bneghalf])

        pst, bps = psr.next()
        mm_group(pst, bps, [(pst[:, 0:96], [(stA[:, :], identf[0:96, 0:96])])], [bstA, bident])
        S.op("dve", lambda e, pst=pst: e.tensor_copy(out=colA[:], in_=pst[:, 0:96]), reads=[bps], writes=[bcolA])
        pst, bps = psr.next()
        mm_group(pst, bps, [(pst[:, 0:82], [(stB[:, :], identf[0:82, 0:82])])], [bstB, bident])
        S.op("dve", lambda e, pst=pst: e.tensor_copy(out=colB[:], in_=pst[:, 0:82]), reads=[bps], writes=[bcolB])
        S.op("act", lambda e: e.activation(out=scT[:].rearrange("p k c -> p c k"),
                                           in_=colB[:, 0:16].rearrange("p (c k) -> p c k", c=2), func=AF.Silu),
             reads=[bcolB], writes=[bscT])
        for half in range(2):
            pst, bps = psr.next()
            mm_group(pst, bps, [(pst[:, k * P:(k + 1) * P], [(wsraw[:, half * 4 + k, :], identf[:, :])]) for k in range(4)],
                     [bwsraw, bident])
            S.op("dve", lambda e, pst=pst, half=half: e.tensor_copy(out=wsT[:, half, :, :], in_=v4(pst[:, :])),
                 reads=[bps], writes=[bwsT])

        NG0, GNG0, PS0, WC0 = 16, 48, 50, 58

        def emit_adaln(l):
            _g, bgbias = t4_r.next()
            gbias = _g[0:2, :]
            _m, bmgate = t4_r.next()
            mgate = _m[0:2, :]
            for pc in range(6):
                S.dma("pool", lambda e, pc=pc: e.dma_start(
                    out=wada[:], in_=wada_d[l, :, pc * 512:(pc + 1) * 512].rearrange("(k p) n -> p k n", p=P)),
                    writes=[bwada])
                pst, bps = psr.next()
                if pc < 4:
                    items = []
                    for c in range(4):
                        items.append((pst[:, c * 2:(c + 1) * 2],
                                      [(wada[:, k, c * P:(c + 1) * P], scT[:, k, :]) for k in range(8)]))
                    mm_group(pst, bps, items, [bwada, bscT])
                    S.op("dve", lambda e, pst=pst, pc=pc: e.tensor_tensor(
                        out=adaC[:, pc * 4:(pc + 1) * 4, :],
                        in0=pst[:, 0:8].rearrange("p (c n) -> p c n", c=4),
                        in1=colA[:, l * 24 + pc * 4:l * 24 + (pc + 1) * 4].unsqueeze(2).to_broadcast([P, 4, 2]),
                        op=ALU.add), reads=[bps, bcolA], writes=[badaC])
                else:
                    n0 = (pc - 4) * 512
                    mm_group(pst, bps, [(pst[0:2, :], [(scT[:, k, :], wada[:, k, :]) for k in range(8)])], [bwada, bscT])
                    if pc == 4:
                        dma_in(gbias, bada_d[l, 2 * D:3 * D].partition_broadcast(2), [bgbias])
                    S.op("dve", lambda e, pst=pst, n0=n0: e.tensor_tensor(
                        out=mgate[:, n0:n0 + 512], in0=pst[0:2, :], in1=gbias[:, n0:n0 + 512], op=ALU.add),
                        reads=[bps, bgbias], writes=[bmgate])
            S.op("dve", lambda e: e.scalar_tensor_tensor(
                out=gsC[:], in0=adaC[:, 8:16, :], scalar=1.0,
                in1=colB[:, NG0 + l * 8:NG0 + (l + 1) * 8].unsqueeze(2).to_broadcast([P, 8, 2]),
                op0=ALU.add, op1=ALU.mult), reads=[badaC, bcolB], writes=[bgsC])
            for c in range(2):
                for n in range(2):
                    pst, bps = psr.next()
                    mm_group(pst, bps, [(pst[:, :], [(sel[:, c, :], mgate[:, n * 512:(n + 1) * 512])])], [bsel, bmgate])
                    S.op("act", lambda e, pst=pst, c=c, n=n: e.activation(
                        out=gateb[c][:, n * 512:(n + 1) * 512], in_=pst[:, :], func=AF.Copy),
                        reads=[bps], writes=[bgateb[c]])

        def load_weights(l):
            j = l // 2
            if l % 2 == 0:
                dma_in(sgugb[:], sng_d[j].partition_broadcast(P), [bsgugb])
                dma_in(bsb[:], bs_d[j].rearrange("h n -> (h n)").partition_broadcast(P), [bbsb])
            src = wie_d[j] if l % 2 == 0 else wio_d[j]
            ncols = EVEN_IN if l % 2 == 0 else ODD_IN
            for k in range(8):
                S.dma("pool", lambda e, k=k, src=src, ncols=ncols: e.dma_start(
                    out=win[:, k, 0:ncols], in_=src[k * P:(k + 1) * P, :]), writes=[bwin[k]])
            for k in range(8):
                S.dma("pool", lambda e, k=k: e.dma_start(
                    out=wout[:, k, :], in_=wout_d[l, k * P:(k + 1) * P, :]), writes=[bwout[k]])

        def x_pieces(dram, base, t, colmajor):
            if not colmajor:
                return [(dram[base + t * P: base + (t + 1) * P, :], 0, P)]
            v = dram[base:base + TS, :].rearrange("(r c) d -> c r d", c=64)
            return [(v[2 * t + cc], cc * 64, (cc + 1) * 64) for cc in range(2)]

        def front_load(l, u, t, colmajor):
            base, T, ci, is_s = units[u]
            if l == 0:
                src, sbase = (xs_d, 0) if is_s else (xp_d, base)
            else:
                src, sbase = xscr[l % 2], base
            xt, bxt = xt_r.next()
            pcs = x_pieces(src, sbase, t, colmajor)
            S.dma("sp", lambda e, pcs=pcs, xt=xt: [e.dma_start(out=xt[lo:hi, :], in_=ap) for ap, lo, hi in pcs],
                  reads=([bxs[l % 2][u]] if l > 0 else []), writes=[bxt], n=len(pcs))
            return (xt, bxt)

        def front_norm_a(ld, ci):
            xt, bxt = ld
            xn, bxn = xn_r.next()
            sm, bsm = sm_r.next()
            S.op("act", lambda e: e.activation(out=xn[:], in_=xt[:], func=AF.Square, accum_out=sm[:, 0:1]),
                 reads=[bxt], writes=[bxn, bsm])
            S.op("dve", lambda e: e.tensor_scalar(out=sm[:, 1:2], in0=sm[:, 0:1], scalar1=1.0 / D, scalar2=EPS, op0=ALU.mult, op1=ALU.add),
                 reads=[bsm], writes=[bsm])
            S.op("pool", lambda e: e.tensor_tensor(out=sm[:, 2:3], in0=sm[:, 1:2], in1=neghalf[:, 0:1], op=ALU.pow),
                 reads=[bsm, bneghalf], writes=[bsm])
            S.op("dve", lambda e: e.tensor_scalar(out=xn[:], in0=xt[:], scalar1=sm[:, 2:3], scalar2=None, op0=ALU.mult),
                 reads=[bxt, bsm], writes=[bxn])
            return (xt, bxt, xn, bxn, ci)

        def front_norm_b(na):
            xt, bxt, xn, bxn, ci = na

            def f_tp(e):
                last = None
                for k in range(8):
                    last = e.transpose(out=tps[:, k * P:(k + 1) * P], in_=xn[:, k * P:(k + 1) * P], identity=identb[:])
                return last
            S.op("pe", f_tp, reads=[bxn, bidentb], writes=[btps])
            hT, bhT = hT_r.next()

            def f_ev(e):
                last = None
                for k in range(8):
                    last = e.tensor_scalar(out=hT[:, k, :], in0=tps[:, k * P:(k + 1) * P],
                                           scalar1=gsC[:, k, ci:ci + 1], scalar2=adaC[:, k, ci:ci + 1],
                                           op0=ALU.mult, op1=ALU.add)
                return last
            S.op("dve", f_ev, reads=[btps, bgsC, badaC], writes=[bhT])
            return (xt, bxt, hT, bhT)

        def proj_fm(hT, bhT, col0, nchunks):
            pst, bps = psr.next()
            items = [(pst[:, c * P:(c + 1) * P],
                      [(win[:, k, col0 + c * P: col0 + (c + 1) * P], hT[:, k, :]) for k in range(8)])
                     for c in range(nchunks)]
            mm_group(pst, bps, items, [bhT] + bwin)
            return pst, bps

        def proj_tm(hT, bhT, col0, ncols):
            pst, bps = psr.next()
            mm_group(pst, bps, [(pst[:, 0:ncols], [(hT[:, k, :], win[:, k, col0:col0 + ncols]) for k in range(8)])],
                     [bhT] + bwin)
            return pst, bps

        def out_proj(l, u, t, colmajor, xt, bxt, mixT, bmix):
            base, T, ci, is_s = units[u]
            t4, bt4 = t4_r.next()
            for n in range(2):
                pst, bps = psr.next()
                mm_group(pst, bps, [(pst[:, :], [(mixT[:, k, :], wout[:, k, n * 512:(n + 1) * 512]) for k in range(8)])],
                         [bmix] + bwout)
                S.op("dve", lambda e, pst=pst, n=n: e.tensor_tensor(
                    out=t4[:, n * 512:(n + 1) * 512], in0=pst[:, :], in1=gateb[ci][:, n * 512:(n + 1) * 512], op=ALU.mult),
                    reads=[bps, bgateb[ci]], writes=[bt4])
                S.op("dve", lambda e, n=n: e.tensor_tensor(
                    out=t4[:, n * 512:(n + 1) * 512], in0=t4[:, n * 512:(n + 1) * 512], in1=xt[:, n * 512:(n + 1) * 512], op=ALU.add),
                    reads=[bt4, bxt], writes=[bt4])
            if l == depth - 1:
                yt, byt = t4_r.next()
                sm, bsm = sm_r.next()
                S.op("act", lambda e: e.activation(out=yt[:], in_=t4[:], func=AF.Square, accum_out=sm[:, 0:1]),
                     reads=[bt4], writes=[byt, bsm])
                S.op("dve", lambda e: e.tensor_scalar(out=sm[:, 1:2], in0=sm[:, 0:1], scalar1=1.0 / D, scalar2=EPS, op0=ALU.mult, op1=ALU.add),
                     reads=[bsm], writes=[bsm])
                S.op("pool", lambda e: e.tensor_tensor(out=sm[:, 2:3], in0=sm[:, 1:2], in1=neghalf[:, 0:1], op=ALU.pow),
                     reads=[bsm, bneghalf], writes=[bsm])
                S.op("act", lambda e: e.activation(out=yt[:], in_=t4[:], func=AF.Copy, scale=sm[:, 2:3]),
                     reads=[bt4, bsm], writes=[byt])
                S.op("pool", lambda e: e.tensor_tensor(out=yt[:], in0=yt[:], in1=fgb[:], op=ALU.mult),
                     reads=[byt, bfgb], writes=[byt])
                dst, dbase = (ys_d, 0) if is_s else (yp_d, base)
                src_t, bsrc = yt, byt
                dwr = []
            else:
                dst, dbase = xscr[(l + 1) % 2], base
                src_t, bsrc = t4, bt4
                dwr = [bxs[(l + 1) % 2][u]]
            pcs = x_pieces(dst, dbase, t, colmajor)
            S.dma("sp", lambda e, pcs=pcs, src_t=src_t: [e.dma_start(out=ap, in_=src_t[lo:hi, :]) for ap, lo, hi in pcs],
                  reads=[bsrc], writes=dwr, n=len(pcs))

        def gla_prepA(l, d, hT, bhT):
            j = l // 2
            jd = j * 2 + d
            pst, bps = psr.next()
            lrc = LRF0 if d == 0 else LRB0
            mm_group(pst, bps, [(pst[0:16, 0:P], [(win[:, k, lrc:lrc + 16], hT[:, k, :]) for k in range(8)])], [bhT] + bwin)
            lrT, blrT = lrT_r.next()
            S.op("dve", lambda e, pst=pst: e.tensor_copy(out=lrT[0:16, :], in_=pst[0:16, 0:P]), reads=[bps], writes=[blrT])
            psv, bpsv = proj_tm(hT, bhT, V0, 512)
            vt, bvt = vt_r.next()
            S.op("dve", lambda e: e.tensor_copy(out=vt[:], in_=psv[:, :]), reads=[bpsv], writes=[bvt])
            psg, bpsg = psr.next()
            mm_group(psg, bpsg, [(psg[:, 0:256], [(lrT[:, :], wgk[:, jd, :])])], [blrT, bwgk])
            eg, beg = eg_r.next()
            S.op("act", lambda e: e.activation(out=eg[:], in_=psg[:, 0:256], func=AF.Exp, scale=-1.0), reads=[bpsg], writes=[beg])
            lg, blg = eg, beg
            S.op("act", lambda e: e.activation(out=lg[:], in_=eg[:], func=AF.Ln, bias=1.0), reads=[beg], writes=[blg])
            return dict(lg=lg, blg=blg, vt=vt, bvt=bvt)

        def gla_prepB(l, d, hT, bhT, pa):
            last = P - 1 if d == 0 else 0
            lg, blg, vt, bvt = pa["lg"], pa["blg"], pa["vt"], pa["bvt"]
            psq, bpsq = proj_fm(hT, bhT, Q0, 4)
            psb, bpsb = psr.next()
            mm_group(psb, bpsb, [(psb[:, c * P:(c + 1) * P], [(lg[:, c * P:(c + 1) * P], tri[:, d, :])]) for c in range(2)],
                     [blg, btri])
            Eq, bEq = Eq_r.next()
            Ek, bEk = Ek_r.next()
            sm, bsm = sm_r.next()
            S.op("act", lambda e: e.activation(out=Eq[:], in_=psb[:, 0:256], func=AF.Exp, bias=math.log(0.125)), reads=[bpsb], writes=[bEq])
            S.op("act", lambda e: e.activation(out=Ek[:], in_=psb[:, 0:256], func=AF.Exp, scale=-1.0), reads=[bpsb], writes=[bEk])
            S.op("act", lambda e: e.activation(out=sm[:, 0:2], in_=psb[:, 0:256].rearrange("p (c n) -> p c n", c=2)[:, :, last],
                                               func=AF.Exp), reads=[bpsb], writes=[bsm])
            qd, bqd = qd_r.next()
            kd, bkd = kd_r.next()
            S.op("dve", lambda e: e.tensor_tensor(out=qd[:], in0=psq[:, 0:256], in1=Eq[:], op=ALU.mult), reads=[bpsq, bEq], writes=[bqd])
            S.op("dve", lambda e: e.tensor_tensor(out=kd[:], in0=psq[:, 256:512], in1=Ek[:], op=ALU.mult), reads=[bpsq, bEk], writes=[bkd])
            def f_tp(e):
                e.transpose(out=tps[:, 0:P], in_=kd[:, 0:P], identity=identb[:])
                return e.transpose(out=tps[:, P:2 * P], in_=kd[:, P:2 * P], identity=identb[:])
            S.op("pe", f_tp, reads=[bkd, bidentb], writes=[btps])
            kdt, bkdt = kdt_r.next()
            S.op("dve", lambda e: e.tensor_copy(out=kdt[:], in_=tps[:, 0:256]), reads=[btps], writes=[bkdt])
            psa0, bpsa0 = psr.next()
            psa1, bpsa1 = psr.next()
            psas = (psa0, psa1)
            items = []
            for h in (0, 2, 1, 3):
                lo = (h % 2) * 64
                cs = (h // 2) * P
                items.append((psas[h % 2][:, (h // 2) * P:(h // 2 + 1) * P], [(kd[lo:lo + 64, cs:cs + P], qd[lo:lo + 64, cs:cs + P])]))

            def fn_att(e, items=items):
                last = None
                for out_ap, pairs in items:
                    for l_, r_ in pairs:
                        last = e.matmul(out_ap, lhsT=l_, rhs=r_, start=True, stop=True)
                return last
            S.op("pe", fn_att, reads=[bkd, bqd], writes=[bpsa0, bpsa1])
            att, batt = att_r.next()
            attv = att[:].rearrange("p (a b) n -> p a b n", b=2)

            def fn_mask(e):
                e.tensor_tensor(out=attv[:, :, 0, :], in0=psa0[:, 0:256].rearrange("p (a n) -> p a n", a=2),
                                in1=mask[:, d, :].unsqueeze(1).to_broadcast([P, 2, P]), op=ALU.mult)
                return e.tensor_tensor(out=attv[:, :, 1, :], in0=psa1[:, 0:256].rearrange("p (a n) -> p a n", a=2),
                                       in1=mask[:, d, :].unsqueeze(1).to_broadcast([P, 2, P]), op=ALU.mult)
            S.op("dve", fn_mask, reads=[bpsa0, bpsa1, bmask], writes=[batt])
            return dict(qd=qd, bqd=bqd, vt=vt, bvt=bvt, att=att, batt=batt, kdt=kdt, bkdt=bkdt, sm=sm, bsm=bsm)

        def gla_scan(l, d, hd):
            qd, bqd, vt, bvt, att, batt = hd["qd"], hd["bqd"], hd["vt"], hd["bvt"], hd["att"], hd["batt"]
            kdt, bkdt, sm, bsm = hd["kdt"], hd["bkdt"], hd["sm"], hd["bsm"]
            pso, bpso = psr.next()
            items = []
            for h in range(4):
                lo = (h % 2) * 64
                cs = (h // 2) * P
                items.append((pso[:, h * P:(h + 1) * P],
                              [(Sbf[d][lo:lo + 64, h // 2, :], qd[lo:lo + 64, cs:cs + P]),
                               (vt[:, h * P:(h + 1) * P], att[:, h, :])]))
            mm_group(pso, bpso, items, [bSbf[d], bqd, bvt, batt])
            pss, bpss = psr.next()
            mm_group(pss, bpss, [(pss[:, p * 256:(p + 1) * 256], [(kdt[:, p * P:(p + 1) * P], vt[:, p * 256:(p + 1) * 256])])
                                 for p in range(2)], [bkdt, bvt])
            tS, btS = tS_r.next()
            pv = pss[:, :].rearrange("p (a n) -> p a n", a=2)

            def f_ts(e):
                e.tensor_tensor(out=tS[0:64, :, :], in0=pv[0:64, :, 0:P], in1=Sst[d][0:64, :, :], op=ALU.add)
                return e.tensor_tensor(out=tS[64:128, :, :], in0=pv[64:128, :, P:2 * P], in1=Sst[d][64:128, :, :], op=ALU.add)
            S.op("dve", f_ts, reads=[bpss, bSst[d]], writes=[btS])

            def f_s(e):
                e.tensor_scalar(out=Sst[d][:, 0, :], in0=tS[:, 0, :], scalar1=sm[:, 0:1], scalar2=None, op0=ALU.mult)
                return e.tensor_scalar(out=Sst[d][:, 1, :], in0=tS[:, 1, :], scalar1=sm[:, 1:2], scalar2=None, op0=ALU.mult)
            S.op("dve", f_s, reads=[btS, bsm], writes=[bSst[d]])

            def f_sb(e):
                e.activation(out=Sbf[d][:, 0, :], in_=tS[:, 0, :], func=AF.Identity, scale=sm[:, 0:1])
                return e.activation(out=Sbf[d][:, 1, :], in_=tS[:, 1, :], func=AF.Identity, scale=sm[:, 1:2])
            S.op("act", f_sb, reads=[btS, bsm], writes=[bSbf[d]])
            return pso, bpso

        def o_scr_ap(base, t):
            return oscr[:, base + t * P: base + (t + 1) * P].rearrange("(h v) n -> v h n", v=P)

        def state_ap(dram5, hh):
            return dram5.rearrange("(p hh) k v -> hh k p v", hh=2)[hh]

        def init_state(l, u, d):
            base, T, ci, is_s = units[u]
            j = l // 2
            for d in (d,):
                if is_s:
                    S.dma("sp", lambda e, d=d: [e.dma_start(out=Sst[d][hh * 64:(hh + 1) * 64, :, :], in_=state_ap(st0_d[j, d], hh))
                                                for hh in range(2)], writes=[bSst[d]], n=2)
                    S.op("act", lambda e, d=d: e.activation(out=Sbf[d][:], in_=Sst[d][:], func=AF.Copy), reads=[bSst[d]], writes=[bSbf[d]])
                else:
                    S.op("pool", lambda e, d=d: e.memset(Sst[d][:], 0.0), writes=[bSst[d]])
                    S.op("pool", lambda e, d=d: e.memset(Sbf[d][:], 0.0), writes=[bSbf[d]])

        def store_state(l, u, d):
            base, T, ci, is_s = units[u]
            if is_s:
                return
            j = l // 2
            S.dma("sp", lambda e: [e.dma_start(out=state_ap(ns_d[u, j, d], hh), in_=Sst[d][hh * 64:(hh + 1) * 64, :, :])
                                   for hh in range(2)], reads=[bSst[d]], n=2)

        def even_pass1_tile(l, u, t, hd, first=False, last=False):
            base = units[u][0]
            if first:
                init_state(l, u, 1)
            pso, bpso = gla_scan(l, 1, hd)
            if last:
                store_state(l, u, 1)
            ob, bob = f2k.next()
            S.op("dve", lambda e: e.tensor_copy(out=ob[:], in_=pso[:, :]), reads=[bpso], writes=[bob])
            S.dma("sp", lambda e: e.dma_start(out=o_scr_ap(base, t), in_=v4(ob[:])), reads=[bob], writes=[bos[u]])

        def even_pass2_tile(l, u, t, colmajor, fr, hd, first=False, last=False, pend=None):
            base, T, ci, is_s = units[u]
            j = l // 2
            xt, bxt, hT, bhT = fr
            orl, borl = f2k.next()
            S.dma("sp", lambda e: e.dma_start(out=v4(orl[:]), in_=o_scr_ap(base, t)), reads=[bos[u]], writes=[borl])
            mixT, bmix = mix_r.next()
            psvs, bpsvs = proj_tm(hT, bhT, VS0, 512)
            vsn, bvsn = vsn_r.next()
            sm, bsm = sm_r.next()
            S.op("act", lambda e: e.activation(out=vsn[:], in_=psvs[:, :], func=AF.Square, accum_out=sm[:, 0:1]),
                 reads=[bpsvs], writes=[bvsn, bsm])
            S.op("dve", lambda e: e.tensor_scalar(out=sm[:, 1:2], in0=sm[:, 0:1], scalar1=1.0 / 512, scalar2=EPS, op0=ALU.mult, op1=ALU.add),
                 reads=[bsm], writes=[bsm])
            S.op("pool", lambda e: e.tensor_tensor(out=sm[:, 2:3], in0=sm[:, 1:2], in1=neghalf[:, 0:1], op=ALU.pow),
                 reads=[bsm, bneghalf], writes=[bsm])
            S.op("dve", lambda e: e.scalar_tensor_tensor(out=vsn[:], in0=psvs[:, :], scalar=sm[:, 2:3], in1=sgugb[:, :],
                                                         op0=ALU.mult, op1=ALU.mult), reads=[bpsvs, bsm, bsgugb], writes=[bvsn])
            psga, bpsga = proj_fm(hT, bhT, GA0, 4)
            sga, bsga = f2k.next()
            S.op("act", lambda e: e.activation(out=sga[:], in_=psga[:, :], func=AF.Silu), reads=[bpsga], writes=[bsga])
            psu, bpsu = proj_fm(hT, bhT, U0, 4)
            usb, busb = f2k.next()
            S.op("act", lambda e: e.activation(out=usb[:], in_=psu[:, :], func=AF.Copy), reads=[bpsu], writes=[busb])
            psgb, bpsgb = proj_fm(hT, bhT, GB0, 4)
            sgb, bsgb = f2k.next()
            S.op("act", lambda e: e.activation(out=sgb[:], in_=psgb[:, :], func=AF.Silu), reads=[bpsgb], writes=[bsgb])
            pssp, bpssp = psr.next()
            mm_group(pssp, bpssp, [(pssp[:, h * P:(h + 1) * P], [(vsn[:, h * P:(h + 1) * P], wsT[:, j, h, :])]) for h in range(4)],
                     [bvsn, bwsT])
            t2, bt2 = f2k.next()
            S.op("dve", lambda e: e.tensor_tensor(out=t2[:], in0=pssp[:, :], in1=bsb[:, :], op=ALU.add), reads=[bpssp, bbsb], writes=[bt2])
            S.op("pool", lambda e: e.tensor_tensor(out=t2[:], in0=t2[:], in1=usb[:], op=ALU.mult), reads=[bt2, busb], writes=[bt2])
            S.op("pool", lambda e: e.tensor_tensor(out=mixT[:, 4:8, :], in0=v4(t2[:]), in1=v4(sgb[:]), op=ALU.mult),
                 reads=[bt2, bsgb], writes=[bmix])
            if pend is not None:
                out_proj(*pend)
            if first:
                init_state(l, u, 0)
            pso, bpso = gla_scan(l, 0, hd)
            if last:
                store_state(l, u, 0)
            ob, bob = f2k.next()
            S.op("dve", lambda e: e.tensor_tensor(out=ob[:], in0=pso[:, :], in1=orl[:], op=ALU.add), reads=[bpso, borl], writes=[bob])
            osqb, bosqb = osqb_r.next()
            S.op("act", lambda e: e.activation(out=osqb[:], in_=ob[:], func=AF.Square), reads=[bob], writes=[bosqb])

            def epilogue():
                osq, bosq = f2k.next()
                psss, bpsss = psr.next()
                mm_group(psss, bpsss, [(psss[:, :], [(ones128[:, :], osqb[:, :])])], [bones, bosqb])
                rso, brso = f2k.next()
                S.op("act", lambda e: e.activation(out=rso[:], in_=psss[:, :], func=AF.Ln, bias=EPS), reads=[bpsss], writes=[brso])
                S.op("act", lambda e: e.activation(out=rso[:], in_=rso[:], func=AF.Exp, scale=-0.5), reads=[brso], writes=[brso])
                S.op("dve", lambda e: e.scalar_tensor_tensor(out=osq[:], in0=ob[:], scalar=colB[:, GNG0 + j:GNG0 + j + 1], in1=rso[:],
                                                             op0=ALU.mult, op1=ALU.mult), reads=[bob, bcolB, brso], writes=[bosq])
                S.op("pool", lambda e: e.tensor_tensor(out=mixT[:, 0:4, :], in0=v4(osq[:]), in1=v4(sga[:]), op=ALU.mult),
                     reads=[bosq, bsga], writes=[bmix])
            return ((l, u, t, colmajor, xt, bxt, mixT, bmix), epilogue)

        def even_layer(l, ulist):
            for pas in (1, 2):
                d = 1 if pas == 1 else 0
                stream = []
                for u in ulist:
                    base, T, ci, is_s = units[u]
                    nu = T // P
                    order = list(range(nu - 1, -1, -1)) if pas == 1 else list(range(nu))
                    cm = is_s and (l // 2) % 2 == 1
                    for k, t in enumerate(order):
                        stream.append((u, t, cm, ci, k == 0, k == nu - 1))
                n = len(stream)

                def LD(k):
                    u, t, cm, ci, f, la = stream[k]
                    return front_load(l, u, t, cm)

                def NA(k, ld):
                    return front_norm_a(ld, stream[k][3])
                A = 4 if pas == 1 else 3
                lds = {k: LD(k) for k in range(min(A, n))}
                nas, frs, pas_, hds = {}, {}, {}, {}
                nas[0] = NA(0, lds[0])
                frs[0] = front_norm_b(nas.pop(0))
                pas_[0] = gla_prepA(l, d, frs[0][2], frs[0][3])
                if n > 1:
                    nas[1] = NA(1, lds[1])
                    frs[1] = front_norm_b(nas.pop(1))
                hds[0] = gla_prepB(l, d, frs[0][2], frs[0][3], pas_.pop(0))
                if n > 1:
                    pas_[1] = gla_prepA(l, d, frs[1][2], frs[1][3])
                if n > 2:
                    nas[2] = NA(2, lds[2])
                pend = None
                epi = None
                for i in range(n):
                    u, t, cm, ci, first, last = stream[i]
                    if i + A < n:
                        lds[i + A] = LD(i + A)
                    if i + 2 < n:
                        frs[i + 2] = front_norm_b(nas.pop(i + 2))
                    if pas == 1 and i + 3 < n:
                        nas[i + 3] = NA(i + 3, lds[i + 3])
                    if i + 1 < n:
                        hds[i + 1] = gla_prepB(l, d, frs[i + 1][2], frs[i + 1][3], pas_.pop(i + 1))
                    if i + 2 < n:
                        pas_[i + 2] = gla_prepA(l, d, frs[i + 2][2], frs[i + 2][3])
                    if epi is not None:
                        epi()
                        epi = None
                    if pas == 2 and i + 3 < n:
                        nas[i + 3] = NA(i + 3, lds[i + 3])
                    if pas == 1:
                        frs.pop(i)
                        even_pass1_tile(l, u, t, hds.pop(i), first, last)
                    else:
                        pend, epi = even_pass2_tile(l, u, t, cm, frs.pop(i), hds.pop(i), first, last, pend)
                if epi is not None:
                    epi()
                if pend is not None:
                    out_proj(*pend)

        def odd_A1(l, u, t, n, fr, k):
            xt, bxt, hT, bhT = fr
            s = k % 2
            sp_ = 1 - s
            pxc, bpxc = proj_fm(hT, bhT, XC0, 4)
            S.op("act", lambda e: e.activation(out=XC[s][:, :, 8:136], in_=v4(pxc[:, :]), func=AF.Copy), reads=[bpxc], writes=[bXC[s]])
            if t == 0:
                S.op("pool", lambda e: e.memset(XC[s][:, :, 0:8], 0.0), writes=[bXC[s]])
            else:
                S.op("pool", lambda e: e.tensor_copy(out=XC[s][:, :, 0:8], in_=XC[sp_][:, :, 128:136]), reads=[bXC[sp_]], writes=[bXC[s]])
                S.op("pool", lambda e: e.tensor_copy(out=XC[sp_][:, :, 136:144], in_=XC[s][:, :, 8:16]), reads=[bXC[s]], writes=[bXC[sp_]])
            if t == n - 1:
                S.op("pool", lambda e: e.memset(XC[s][:, :, 136:144], 0.0), writes=[bXC[s]])

        def odd_A2(l, u, t, n, fr, held, k):
            xt, bxt, hT, bhT = fr
            s = k % 2
            sp_ = 1 - s
            pxd, bpxd = proj_fm(hT, bhT, XD0, 4)
            xd, bxd = f2k.next()
            S.op("act", lambda e: e.activation(out=xd[:], in_=pxd[:, :], func=AF.Copy), reads=[bpxd], writes=[bxd])
            pcd, bpcd = proj_fm(hT, bhT, CD0, 4)
            S.op("dve", lambda e: e.tensor_tensor(out=UP[s][:, :, 1:129], in0=v4(pcd[:, :]), in1=v4(xd[:]), op=ALU.mult),
                 reads=[bpcd, bxd], writes=[bUP[s]])
            if t == 0:
                S.op("pool", lambda e: e.memset(UP[s][:, :, 0:1], 0.0), writes=[bUP[s]])
            else:
                S.op("dve", lambda e: e.tensor_copy(out=UP[s][:, :, 0:1], in_=UP[sp_][:, :, 128:129]), reads=[bUP[sp_]], writes=[bUP[s]])
                S.op("dve", lambda e: e.tensor_copy(out=UP[sp_][:, :, 129:130], in_=UP[s][:, :, 1:2]), reads=[bUP[s]], writes=[bUP[sp_]])
            if t == n - 1:
                S.op("pool", lambda e: e.memset(UP[s][:, :, 129:130], 0.0), writes=[bUP[s]])
            pbd, bpbd = proj_fm(hT, bhT, BD0, 4)
            bd, bbd = hold_r.next()
            S.op("act", lambda e: e.activation(out=bd[:], in_=pbd[:, :], func=AF.Copy), reads=[bpbd], writes=[bbd])
            pgd, bpgd = proj_fm(hT, bhT, GD0, 4)
            sgd, bsgd = hold_r.next()
            S.op("act", lambda e: e.activation(out=sgd[:], in_=pgd[:, :], func=AF.Silu), reads=[bpgd], writes=[bsgd])
            pgc, bpgc = proj_fm(hT, bhT, GC0, 4)
            sgc, bsgc = hold_r.next()
            S.op("act", lambda e: e.activation(out=sgc[:], in_=pgc[:, :], func=AF.Silu), reads=[bpgc], writes=[bsgc])
            held[k] = (xt, bxt, sgc, bsgc, bd, bbd, sgd, bsgd)

        def odd_Bpool(l, u, t, n, k):
            s = k % 2
            X = XC[s]
            S.op("pool", lambda e: e.tensor_tensor(out=L1[:], in0=X[:, :, 1:144], in1=X[:, :, 0:143], op=ALU.add), reads=[bXC[s]], writes=[bL1])
            S.op("pool", lambda e: e.tensor_tensor(out=L2[:], in0=L1[:, 1:4, 0:141], in1=L1[:, 1:4, 2:143], op=ALU.add), reads=[bL1], writes=[bL2])
            S.op("pool", lambda e: e.tensor_tensor(out=L3[:], in0=L2[:, 1:3, 0:137], in1=L2[:, 1:3, 4:141], op=ALU.add), reads=[bL2], writes=[bL3])
            S.op("pool", lambda e: e.tensor_tensor(out=L4[:], in0=L3[:, 1, 0:128], in1=L3[:, 1, 8:136], op=ALU.add), reads=[bL3], writes=[bL4])
            var = 0 if t == 0 else (2 if t == n - 1 else 1)
            pooled, bpooled = pooled_r.next()
            srcs = (L1[:, 0, 7:135], L2[:, 0, 6:134], L3[:, 0, 4:132], L4[:, :])
            if var == 1:
                def f_pl(e):
                    last = None
                    for g, w in enumerate((2, 4, 8, 16)):
                        last = e.scalar_tensor_tensor(out=pooled[:, g, :], in0=srcs[g], scalar=1.0 / w, in1=X[:, g, 8:136],
                                                      op0=ALU.mult, op1=ALU.subtract)
                    return last
                S.op("dve", f_pl, reads=[bL1, bL2, bL3, bL4, bXC[s]], writes=[bpooled])
            else:
                pm, bpm = f2k.next()
                pmv = v4(pm[:])

                def f_pm(e):
                    last = None
                    for g in range(4):
                        last = e.tensor_tensor(out=pmv[:, g, :], in0=srcs[g], in1=rc[:, (0 if var == 0 else 1), g, :], op=ALU.mult)
                    return last
                S.op("dve", f_pm, reads=[bL1, bL2, bL3, bL4, brc], writes=[bpm])
                S.op("dve", lambda e: e.tensor_tensor(out=pooled[:], in0=pmv, in1=X[:, :, 8:136], op=ALU.subtract),
                     reads=[bpm, bXC[s]], writes=[bpooled])
            return (pooled, bpooled)

        def odd_Bconv(l, u, t, held, k):
            j = l // 2
            s = k % 2
            xt, bxt, sgc, bsgc, bd, bbd, sgd, bsgd = held[k]
            U = UP[s]
            y, by = f2k.next()
            yv = v4(y[:])

            def wc(k, c):
                o = WC0 + (j * 3 + k) * 4 + c
                return colB[:, o:o + 1]

            def f_c0(e):
                last = None
                for c in range(4):
                    last = e.tensor_scalar(out=yv[:, c, :], in0=U[:, c, 0:128], scalar1=wc(0, c), scalar2=None, op0=ALU.mult)
                return last
            S.op("dve", f_c0, reads=[bUP[s], bcolB], writes=[by])
            for k in (1, 2):
                def f_ck(e, k=k):
                    last = None
                    for c in range(4):
                        last = e.scalar_tensor_tensor(out=yv[:, c, :], in0=U[:, c, k:k + 128], scalar=wc(k, c), in1=yv[:, c, :],
                                                      op0=ALU.mult, op1=ALU.add)
                    return last
                S.op("dve", f_ck, reads=[bUP[s], bcolB, by], writes=[by])
            S.op("dve", lambda e: e.tensor_tensor(out=y[:], in0=y[:], in1=bd[:], op=ALU.mult), reads=[by, bbd], writes=[by])
            mixT, bmix = mix_r.next()
            S.op("dve", lambda e: e.tensor_tensor(out=mixT[:, 4:8, :], in0=yv, in1=v4(sgd[:]), op=ALU.mult),
                 reads=[by, bsgd], writes=[bmix])
            return (mixT, bmix)

        def odd_Bfin(l, u, t, colmajor, held, pl, mx, k):
            j = l // 2
            xt, bxt, sgc, bsgc, bd, bbd, sgd, bsgd = held.pop(k)
            pooled, bpooled = pl
            mixT, bmix = mx
            ppo, bppo = psr.next()
            mm_group(ppo, bppo, [(ppo[:, g * P:(g + 1) * P], [(wpool[:, j, g, :], pooled[:, g, :])]) for g in range(4)],
                     [bwpool, bpooled])

            def f_oc(e):
                last = None
                for g in range(4):
                    last = e.scalar_tensor_tensor(out=mixT[:, g, :], in0=ppo[:, g * P:(g + 1) * P],
                                                  scalar=colB[:, PS0 + j * 4 + g:PS0 + j * 4 + g + 1],
                                                  in1=sgc[:, g * P:(g + 1) * P], op0=ALU.mult, op1=ALU.mult)
                return last
            S.op("dve", f_oc, reads=[bppo, bcolB, bsgc], writes=[bmix])
            return (l, u, t, colmajor, xt, bxt, mixT, bmix)

        def odd_layer(l, ulist):
            stream = []
            for u in ulist:
                base, T, ci, is_s = units[u]
                nu = T // P
                cm = is_s and (l // 2) % 2 == 1
                for t in range(nu):
                    stream.append((u, t, nu, cm, ci))
            n = len(stream)
            held = {}

            def LD(k):
                u, t, nu, cm, ci = stream[k]
                return front_load(l, u, t, cm)

            def BP(k):
                u, t, nu, cm, ci = stream[k]
                return odd_Bpool(l, u, t, nu, k)

            def BC(k, pl):
                u, t, nu, cm, ci = stream[k]
                mx = odd_Bconv(l, u, t, held, k)
                return odd_Bfin(l, u, t, cm, held, pl, mx, k)
            lds = {0: LD(0)}
            if n > 1:
                lds[1] = LD(1)
            nas = {0: front_norm_a(lds[0], stream[0][4])}
            frs = {0: front_norm_b(nas.pop(0))}
            if n > 1:
                nas[1] = front_norm_a(lds[1], stream[1][4])
            pend = None
            for k in range(n):
                u, t, nu, cm, ci = stream[k]
                if k + 2 < n:
                    lds[k + 2] = LD(k + 2)
                if k + 1 < n:
                    frs[k + 1] = front_norm_b(nas.pop(k + 1))
                fr = frs.pop(k)
                odd_A1(l, u, t, nu, fr, k)
                if k + 2 < n:
                    nas[k + 2] = front_norm_a(lds[k + 2], stream[k + 2][4])
                if pend is not None:
                    out_proj(*pend)
                    pend = None
                pl = BP(k - 1) if k >= 1 else None
                odd_A2(l, u, t, nu, fr, held, k)
                if k >= 1:
                    pend = BC(k - 1, pl)
            if pend is not None:
                out_proj(*pend)
            pl = BP(n - 1)
            out_proj(*BC(n - 1, pl))

        for l in range(depth):
            load_weights(l)
            emit_adaln(l)
            ulist = list(usel if usel is not None else range(len(units)))
            if l % 2 == 0:
                even_layer(l, ulist)
            else:
                odd_layer(l, ulist)
        S.finish()
        S.emit(nc, st)
    return nc


def _constants():
    jj, ii = np.meshgrid(np.arange(P), np.arange(P), indexing="ij")
    le = (jj <= ii).astype(np.float32)
    ge = (jj >= ii).astype(np.float32)
    tri = np.stack([le * (-1.0 / 16.0), ge * (-1.0 / 16.0)]).astype(np.float32)
    mask = np.stack([le, ge]).astype(np.float32)
    rc = np.zeros((2, 4, P), np.float32)
    pos = np.arange(P)
    for g, w in enumerate((2, 4, 8, 16)):
        a = w // 2
        rc[0, g] = 1.0 / (np.minimum(pos, a) + a)
        rc[1, g] = 1.0 / (a + np.minimum(a, P - pos))
    sel = np.zeros((2, 2, P), np.float32)
    sel[0, 0, :] = 1.0
    sel[1, 1, :] = 1.0
    return dict(c_ident=np.eye(P, dtype=np.float32), c_tri=tri, c_mask=mask, c_rc=rc.reshape(-1), c_sel=sel,
                c_ones=np.ones((1, P), np.float32))


_NC_CACHE = {}


def make_in_maps(inputs, n_cores=8):
    f = lambda a: np.ascontiguousarray(np.asarray(a, dtype=np.float32))
    consts = _constants()
    shared = {k: f(inputs[k]) for k in ("w_ada", "b_ada", "norm_g", "w_in_even", "w_in_odd", "w_out", "w_gk", "b_gk",
                                        "gla_norm_g", "sgu_norm_g", "w_s", "b_s", "w_pool", "pool_scale", "w_conv",
                                        "final_norm_g")}
    xp = f(inputs["x_prompt"])
    xs = f(inputs["x_sample"])
    c = f(inputs["c"])
    cctx = f(inputs["c_ctx"])
    stg = f(inputs["state_gla"])
    maps = []
    for core in range(n_cores):
        b = (core // 2) % xs.shape[0]
        m = dict(shared)
        m.update(consts)
        m["xp"] = np.ascontiguousarray(xp[core * NPROMPT:(core + 1) * NPROMPT].reshape(NPROMPT * TP, D))
        m["xs"] = np.ascontiguousarray(xs[b])
        m["cond"] = np.ascontiguousarray(np.stack([cctx, c[b]]).reshape(16, P))
        m["st0"] = np.ascontiguousarray(stg[b])
        maps.append(m)
    return maps


def kernel(x_prompt, x_sample, c, state_gla, c_ctx, w_ada, b_ada, norm_g, w_in_even, w_in_odd, w_out, w_gk, b_gk,
           gla_norm_g, sgu_norm_g, w_s, b_s, w_pool, pool_scale, w_conv, final_norm_g):
    inputs = dict(x_prompt=x_prompt, x_sample=x_sample, c=c, state_gla=state_gla, c_ctx=c_ctx, w_ada=w_ada, b_ada=b_ada,
                  norm_g=norm_g, w_in_even=w_in_even, w_in_odd=w_in_odd, w_out=w_out, w_gk=w_gk, b_gk=b_gk,
                  gla_norm_g=gla_norm_g, sgu_norm_g=sgu_norm_g, w_s=w_s, b_s=b_s, w_pool=w_pool, pool_scale=pool_scale,
                  w_conv=w_conv, final_norm_g=final_norm_g)
    n_cores = 8
    maps = make_in_maps(inputs, n_cores)
    if "nc" not in _NC_CACHE:
        _NC_CACHE["nc"] = build_nc(4)
    res = run_bass_kernel_spmd(_NC_CACHE["nc"], maps, core_ids=list(range(n_cores)))
    r = res.results
    y_prompt = np.concatenate([r[i]["yp"].reshape(NPROMPT, TP, D) for i in range(n_cores)], axis=0).astype(np.float32)
    y_sample = np.stack([r[2 * b]["ys"] for b in range(4)], axis=0).astype(np.float32)
    new_state = np.concatenate([r[i]["ns"] for i in range(n_cores)], axis=0).astype(np.float32)
    return (y_prompt, y_sample, new_state)
```

```python
import contextlib
import math
import numpy as np
import concourse.bass as bass
import concourse.mybir as mybir
from concourse.bass_utils import run_bass_kernel_spmd

F32 = mybir.dt.float32
BF16 = mybir.dt.bfloat16
AF = mybir.ActivationFunctionType
ALU = mybir.AluOpType

D = 1024
P = 128
NPROMPT = 4
TP = 256
TS = 4096
EVEN_IN = 3104
ODD_IN = 3072
Q0, K0, V0, GA0, LRF0, LRB0, U0, VS0, GB0 = 0, 256, 512, 1024, 1536, 1552, 1568, 2080, 2592
XC0, GC0, XD0, BD0, CD0, GD0 = 0, 512, 1024, 1536, 2048, 2560
EPS = 1e-6


class Buf:
    __slots__ = ("name", "lw", "rd")

    def __init__(self, name):
        self.name = name
        self.lw = None
        self.rd = {}


class Sched:
    ENGS = ("sp", "act", "dve", "pool", "pe")
    K = 8

    def __init__(self):
        self.ops = {e: [] for e in self.ENGS}
        self.cnt = {e: 0 for e in self.ENGS}
        self.waited = {e: {} for e in self.ENGS}
        self.dq = {e: {"n": 0, "tgt": [0] * self.K, "last": [None] * self.K} for e in self.ENGS}
        self.semkeys = set()

    def _deps(self, reads, writes):
        deps = []
        for b in reads:
            if b.lw is not None:
                deps.append(b.lw)
        for b in writes:
            if b.lw is not None:
                deps.append(b.lw)
            deps.extend(b.rd.items())
        return deps

    def _filter(self, eng, deps):
        w = self.waited[eng]
        out = {}
        for k, v in deps:
            if w.get(k, 0) >= v:
                continue
            if out.get(k, 0) < v:
                out[k] = v
        for k, v in out.items():
            w[k] = v
        return list(out.items())

    def _mark(self, ident, reads, writes):
        k, v = ident
        for b in reads:
            if b.rd.get(k, 0) < v:
                b.rd[k] = v
        for b in writes:
            b.lw = ident
            b.rd = {}

    def op(self, eng, fn, reads=(), writes=()):
        deps = self._deps(reads, writes)
        if eng == "pe":
            deps = [d for d in deps if d[0] != ("c", "pe")]
        waits = self._filter(eng, deps)
        self.cnt[eng] += 1
        key = ("c", eng)
        self.semkeys.add(key)
        self.ops[eng].append((waits, fn, key, 1))
        self._mark((key, self.cnt[eng]), reads, writes)

    def dma(self, q, fn, reads=(), writes=(), n=1):
        d = self.dq[q]
        slot = d["n"] % self.K
        d["n"] += 1
        deps = self._deps(reads, writes)
        if d["last"][slot] is not None:
            deps.append(d["last"][slot])
        waits = self._filter(q, deps)
        d["tgt"][slot] += 16 * n
        key = ("d", q, slot)
        self.semkeys.add(key)
        ident = (key, d["tgt"][slot])
        d["last"][slot] = ident
        self.ops[q].append((waits, fn, key, 16))
        self._mark(ident, reads, writes)

    def finish(self, eng="sp"):
        deps = []
        for q in self.ENGS:
            for ident in self.dq[q]["last"]:
                if ident is not None:
                    deps.append(ident)
        waits = self._filter(eng, deps)
        self.ops[eng].append((waits, None, None, 0))

    def emit(self, nc, stack):
        sems = {}
        for k in sorted(self.semkeys, key=str):
            sems[k] = stack.enter_context(nc.semaphore("s_" + "_".join(str(x) for x in k)))
        ops = self.ops

        def replay(name, e):
            for waits, fn, key, amt in ops[name]:
                for k, v in waits:
                    e.wait_ge(sems[k], v)
                if fn is None:
                    continue
                ins = fn(e)
                if isinstance(ins, (list, tuple)):
                    for i in ins:
                        i.then_inc(sems[key], amt)
                else:
                    ins.then_inc(sems[key], amt)

        with nc.Block() as block:
            @block.sync
            def _(e):
                replay("sp", e)

            @block.scalar
            def _(e):
                replay("act", e)

            @block.vector
            def _(e):
                replay("dve", e)

            @block.gpsimd
            def _(e):
                replay("pool", e)

            @block.tensor
            def _(e):
                replay("pe", e)


class Ring:
    def __init__(self, tiles):
        self.tiles = tiles
        self.bufs = [Buf("r") for _ in tiles]
        self.i = 0

    def next(self):
        k = self.i % len(self.tiles)
        self.i += 1
        return self.tiles[k], self.bufs[k]


def build_nc(depth=4, usel=None):
    nc = bass.Bass("TRN2", target_bir_lowering=False)
    S = Sched()

    def din(name, shape):
        return nc.dram_tensor(name, list(shape), F32, kind="ExternalInput").ap()

    xp_d = din("xp", [NPROMPT * TP, D])
    xs_d = din("xs", [TS, D])
    cond_d = din("cond", [16, P])
    st0_d = din("st0", [2, 2, 4, 64, 128])
    wada_d = din("w_ada", [4, D, 3 * D])
    bada_d = din("b_ada", [4, 3 * D])
    ng_d = din("norm_g", [4, D])
    wie_d = din("w_in_even", [2, D, EVEN_IN])
    wio_d = din("w_in_odd", [2, D, ODD_IN])
    wout_d = din("w_out", [4, D, D])
    wgk_d = din("w_gk", [2, 2, 16, 256])
    bgk_d = din("b_gk", [2, 2, 256])
    gng_d = din("gla_norm_g", [2, 128])
    sng_d = din("sgu_norm_g", [2, 512])
    ws_d = din("w_s", [2, 4, 128, 128])
    bs_d = din("b_s", [2, 4, 128])
    wpool_d = din("w_pool", [2, 4, 128, 128])
    pscale_d = din("pool_scale", [2, 512])
    wconv_d = din("w_conv", [2, 3, 512])
    fng_d = din("final_norm_g", [D])
    ident_d = din("c_ident", [P, P])
    tri_d = din("c_tri", [2, P, P])
    mask_d = din("c_mask", [2, P, P])
    rc_d = din("c_rc", [2 * 4 * 128])
    sel_d = din("c_sel", [2, 2, P])
    ones_d = din("c_ones", [1, P])

    yp_d = nc.dram_tensor("yp", [NPROMPT * TP, D], F32, kind="ExternalOutput").ap()
    ys_d = nc.dram_tensor("ys", [TS, D], F32, kind="ExternalOutput").ap()
    ns_d = nc.dram_tensor("ns", [NPROMPT, 2, 2, 4, 64, 128], F32, kind="ExternalOutput").ap()

    NTOK = NPROMPT * TP + TS
    xscr = [nc.dram_tensor(f"xscr{i}", [NTOK, D], F32, kind="Internal").ap() for i in range(2)]
    oscr = nc.dram_tensor("oscr", [512, NTOK], F32, kind="Internal").ap()

    units = [(u * TP, TP, 0, False) for u in range(NPROMPT)] + [(NPROMPT * TP, TS, 1, True)]
    bxs = [[Buf(f'xscr{i}_{u}') for u in range(len(units))] for i in range(2)]
    bos = [Buf(f'oscr_{u}') for u in range(len(units))]

    with contextlib.ExitStack() as st:
        def sb(name, shape, dt=F32):
            return st.enter_context(nc.sbuf_tensor(name, list(shape), dt))

        def ps(name, shape, dt=F32):
            return st.enter_context(nc.psum_tensor(name, list(shape), dt))

        def ring(name, shape, dt, n):
            return Ring([sb(f"{name}{i}", shape, dt) for i in range(n)])

        win = sb("win", [P, 8, EVEN_IN], BF16)
        bwin = [Buf(f"win{k}") for k in range(8)]
        wout = sb("wout", [P, 8, D], BF16)
        bwout = [Buf(f"wout{k}") for k in range(8)]
        wada = sb("wada", [P, 8, 512], BF16)
        bwada = Buf("wada")

        xt_r = ring("xt", [P, D], F32, 5)
        xn_r = ring("xn", [P, D], BF16, 1)
        hT_r = ring("hT", [P, 8, P], BF16, 3)
        t4_r = ring("t4", [P, D], F32, 2)
        mix_r = ring("mixT", [P, 8, P], BF16, 2)
        f2k = ring("f2k", [P, 512], F32, 8)
        hold_r = ring("hold", [P, 512], F32, 6)
        sm_r = ring("sm", [P, 4], F32, 8)

        psr = Ring([ps(f"psr{i}", [P, 512], F32) for i in range(7)])
        tps = ps("tps", [P, 1024], BF16)
        btps = Buf("tps")

        identf = sb("identf", [P, P]); bident = Buf("identf")
        identb = sb("identb", [P, P], BF16); bidentb = Buf("identb")
        tri = sb("tri", [P, 2, P]); btri = Buf("tri")
        mask = sb("mask", [P, 2, P]); bmask = Buf("mask")
        rc = sb("rc", [P, 2, 4, P]); brc = Buf("rc")
        sel = sb("sel", [2, 2, P]); bsel = Buf("sel")
        ones128 = sb("ones128", [P, P], BF16); bones = Buf("ones128")
        osqb_r = ring("osqb", [P, 512], BF16, 1)
        fgb = sb("fgb", [P, D]); bfgb = Buf("fgb")
        neghalf = sb("neghalf", [P, 1]); bneghalf = Buf("neghalf")
        stA = sb("stA", [96, P]); bstA = Buf("stA")
        stB = sb("stB", [82, P]); bstB = Buf("stB")
        colA = sb("colA", [P, 96]); bcolA = Buf("colA")
        colB = sb("colB", [P, 82]); bcolB = Buf("colB")
        scT = sb("scT", [P, 8, 2], BF16); bscT = Buf("scT")
        adaC = sb("adaC", [P, 16, 2]); badaC = Buf("adaC")
        gsC = sb("gsC", [P, 8, 2]); bgsC = Buf("gsC")
        gateb = [sb(f"gateb{c}", [P, D]) for c in range(2)]
        bgateb = [Buf(f"gateb{c}") for c in range(2)]
        wsT = sb("wsT", [P, 2, 4, P], BF16); bwsT = Buf("wsT")
        wpool = sb("wpool", [P, 2, 4, P], BF16); bwpool = Buf("wpool")
        wgk = sb("wgk", [17, 4, 256]); bwgk = Buf("wgk")
        sgugb = sb("sgugb", [P, 512]); bsgugb = Buf("sgugb")
        bsb = sb("bsb", [P, 512]); bbsb = Buf("bsb")

        lrT_r = ring("lrT", [17, P], F32, 1)
        eg_r = ring("eg", [P, 256], F32, 1)
        Eq_r = ring("Eq", [P, 256], F32, 1)
        Ek_r = ring("Ek", [P, 256], F32, 1)
        qd_r = ring("qdT", [P, 256], BF16, 2)
        kd_r = ring("kdT", [P, 256], BF16, 1)
        kdt_r = ring("kdtok", [P, 256], BF16, 2)
        vt_r = ring("vtok", [P, 512], BF16, 3)
        att_r = ring("attT", [P, 4, P], BF16, 2)
        tS_r = ring("tS", [P, 2, P], F32, 1)
        Sst = [sb(f"Sst{d}", [P, 2, P]) for d in range(2)]
        bSst = [Buf(f"Sst{d}") for d in range(2)]
        Sbf = [sb(f"Sbf{d}", [P, 2, P], BF16) for d in range(2)]
        bSbf = [Buf(f"Sbf{d}") for d in range(2)]
        vsn_r = ring("vsn", [P, 512], BF16, 1)
        XC = [sb(f"XC{i}", [P, 4, 144]) for i in range(2)]
        bXC = [Buf(f"XC{i}") for i in range(2)]
        UP = [sb(f"UP{i}", [P, 4, 130]) for i in range(2)]
        bUP = [Buf(f"UP{i}") for i in range(2)]
        L1 = sb("L1", [P, 4, 143]); bL1 = Buf("L1")
        L2 = sb("L2", [P, 3, 141]); bL2 = Buf("L2")
        L3 = sb("L3", [P, 2, 137]); bL3 = Buf("L3")
        L4 = sb("L4", [P, P]); bL4 = Buf("L4")
        pooled_r = ring("pooled", [P, 4, P], BF16, 1)

        def mm_group(pst, bps, items, reads):
            def fn(e, items=items):
                last = None
                for out_ap, pairs in items:
                    n = len(pairs)
                    for i, (l, r) in enumerate(pairs):
                        last = e.matmul(out_ap, lhsT=l, rhs=r, start=(i == 0), stop=(i == n - 1))
                return last
            S.op("pe", fn, reads=reads, writes=[bps])

        def v4(ap):
            return ap.rearrange("p (a b) -> p a b", a=4)

        def dma_in(out, in_, writes, q="sp", reads=()):
            S.dma(q, lambda e: e.dma_start(out=out, in_=in_), reads=reads, writes=writes)

        dma_in(identf[:], ident_d, [bident])
        dma_in(tri[:], tri_d.rearrange("d j i -> j d i"), [btri])
        dma_in(mask[:], mask_d.rearrange("d j i -> j d i"), [bmask])
        dma_in(rc[:].rearrange("p a g n -> p (a g n)"), rc_d.partition_broadcast(P), [brc])
        dma_in(sel[:], sel_d.rearrange("c r n -> r c n"), [bsel])
        dma_in(fgb[:], fng_d.partition_broadcast(P), [bfgb])
        dma_in(stA[:], bada_d.rearrange("l (c p) -> (l c) p", p=P), [bstA])
        dma_in(stB[0:16, :], cond_d, [bstB])
        dma_in(stB[16:48, :], ng_d.rearrange("l (c p) -> (l c) p", p=P), [bstB])
        dma_in(stB[48:50, :], gng_d, [bstB])
        dma_in(stB[50:58, :], pscale_d.rearrange("j (c p) -> (j c) p", p=P), [bstB])
        dma_in(stB[58:82, :], wconv_d.rearrange("j k (c p) -> (j k c) p", p=P), [bstB])
        dma_in(wgk[0:16, :, :], wgk_d.rearrange("j d r n -> r (j d) n"), [bwgk])
        dma_in(wgk[16:17, :, :], bgk_d.rearrange("(o j) d n -> o (j d) n", o=1), [bwgk])
        for _t, _b in zip(lrT_r.tiles, lrT_r.bufs):
            dma_in(_t[16:17, :], ones_d, [_b])
        _wsr, bwsraw = t4_r.next()
        wsraw = _wsr[:].rearrange("p (a b) -> p a b", a=8)
        dma_in(wsraw, ws_d.rearrange("j h i k -> i (j h) k"), [bwsraw])
        dma_in(wpool[:].rearrange("p j g n -> p (j g) n"), wpool_d.rearrange("j g c n -> c (j g) n"), [bwpool], q="pool")
        S.op("dve", lambda e: e.tensor_copy(out=identb[:], in_=identf[:]), reads=[bident], writes=[bidentb])
        S.op("pool", lambda e: e.memset(ones128[:], 1.0 / 128.0), writes=[bones])
        S.op("pool", lambda e: e.memset(neghalf[:], -0.5), writes=[bneghalf])

        pst, bps = psr.next()
        mm_group(pst, bps, [(pst[:, 0:96], [(stA[:, :], identf[0:96, 0:96])])], [bstA, bident])
        S.op("dve", lambda e, pst=pst: e.tensor_copy(out=colA[:], in_=pst[:, 0:96]), reads=[bps], writes=[bcolA])
        pst, bps = psr.next()
        mm_group(pst, bps, [(pst[:, 0:82], [(stB[:, :], identf[0:82, 0:82])])], [bstB, bident])
        S.op("dve", lambda e, pst=pst: e.tensor_copy(out=colB[:], in_=pst[:, 0:82]), reads=[bps], writes=[bcolB])
        S.op("act", lambda e: e.activation(out=scT[:].rearrange("p k c -> p c k"),
                                           in_=colB[:, 0:16].rearrange("p (c k) -> p c k", c=2), func=AF.Silu),
             reads=[bcolB], writes=[bscT])
        for half in range(2):
            pst, bps = psr.next()
            mm_group(pst, bps, [(pst[:, k * P:(k + 1) * P], [(wsraw[:, half * 4 + k, :], identf[:, :])]) for k in range(4)],
                     [bwsraw, bident])
            S.op("dve", lambda e, pst=pst, half=half: e.tensor_copy(out=wsT[:, half, :, :], in_=v4(pst[:, :])),
                 reads=[bps], writes=[bwsT])

        NG0, GNG0, PS0, WC0 = 16, 48, 50, 58

        def emit_adaln(l):
            _g, bgbias = t4_r.next()
            gbias = _g[0:2, :]
            _m, bmgate = t4_r.next()
            mgate = _m[0:2, :]
            for pc in range(6):
                S.dma("pool", lambda e, pc=pc: e.dma_start(
                    out=wada[:], in_=wada_d[l, :, pc * 512:(pc + 1) * 512].rearrange("(k p) n -> p k n", p=P)),
                    writes=[bwada])
                pst, bps = psr.next()
                if pc < 4:
                    items = []
                    for c in range(4):
                        items.append((pst[:, c * 2:(c + 1) * 2],
                                      [(wada[:, k, c * P:(c + 1) * P], scT[:, k, :]) for k in range(8)]))
                    mm_group(pst, bps, items, [bwada, bscT])
                    S.op("dve", lambda e, pst=pst, pc=pc: e.tensor_tensor(
                        out=adaC[:, pc * 4:(pc + 1) * 4, :],
                        in0=pst[:, 0:8].rearrange("p (c n) -> p c n", c=4),
                        in1=colA[:, l * 24 + pc * 4:l * 24 + (pc + 1) * 4].unsqueeze(2).to_broadcast([P, 4, 2]),
                        op=ALU.add), reads=[bps, bcolA], writes=[badaC])
                else:
                    n0 = (pc - 4) * 512
                    mm_group(pst, bps, [(pst[0:2, :], [(scT[:, k, :], wada[:, k, :]) for k in range(8)])], [bwada, bscT])
                    if pc == 4:
                        dma_in(gbias, bada_d[l, 2 * D:3 * D].partition_broadcast(2), [bgbias])
                    S.op("dve", lambda e, pst=pst, n0=n0: e.tensor_tensor(
                        out=mgate[:, n0:n0 + 512], in0=pst[0:2, :], in1=gbias[:, n0:n0 + 512], op=ALU.add),
                        reads=[bps, bgbias], writes=[bmgate])
            S.op("dve", lambda e: e.scalar_tensor_tensor(
                out=gsC[:], in0=adaC[:, 8:16, :], scalar=1.0,
                in1=colB[:, NG0 + l * 8:NG0 + (l + 1) * 8].unsqueeze(2).to_broadcast([P, 8, 2]),
                op0=ALU.add, op1=ALU.mult), reads=[badaC, bcolB], writes=[bgsC])
            for c in range(2):
                for n in range(2):
                    pst, bps = psr.next()
                    mm_group(pst, bps, [(pst[:, :], [(sel[:, c, :], mgate[:, n * 512:(n + 1) * 512])])], [bsel, bmgate])
                    S.op("act", lambda e, pst=pst, c=c, n=n: e.activation(
                        out=gateb[c][:, n * 512:(n + 1) * 512], in_=pst[:, :], func=AF.Copy),
                        reads=[bps], writes=[bgateb[c]])

        def load_weights(l):
            j = l // 2
            if l % 2 == 0:
                dma_in(sgugb[:], sng_d[j].partition_broadcast(P), [bsgugb])
                dma_in(bsb[:], bs_d[j].rearrange("h n -> (h n)").partition_broadcast(P), [bbsb])
            src = wie_d[j] if l % 2 == 0 else wio_d[j]
            ncols = EVEN_IN if l % 2 == 0 else ODD_IN
            for k in range(8):
                S.dma("pool", lambda e, k=k, src=src, ncols=ncols: e.dma_start(
                    out=win[:, k, 0:ncols], in_=src[k * P:(k + 1) * P, :]), writes=[bwin[k]])
            for k in range(8):
                S.dma("pool", lambda e, k=k: e.dma_start(
                    out=wout[:, k, :], in_=wout_d[l, k * P:(k + 1) * P, :]), writes=[bwout[k]])

        def x_pieces(dram, base, t, colmajor):
            if not colmajor:
                return [(dram[base + t * P: base + (t + 1) * P, :], 0, P)]
            v = dram[base:base + TS, :].rearrange("(r c) d -> c r d", c=64)
            return [(v[2 * t + cc], cc * 64, (cc + 1) * 64) for cc in range(2)]

        def front_load(l, u, t, colmajor):
            base, T, ci, is_s = units[u]
            if l == 0:
                src, sbase = (xs_d, 0) if is_s else (xp_d, base)
            else:
                src, sbase = xscr[l % 2], base
            xt, bxt = xt_r.next()
            pcs = x_pieces(src, sbase, t, colmajor)
            S.dma("sp", lambda e, pcs=pcs, xt=xt: [e.dma_start(out=xt[lo:hi, :], in_=ap) for ap, lo, hi in pcs],
                  reads=([bxs[l % 2][u]] if l > 0 else []), writes=[bxt], n=len(pcs))
            return (xt, bxt)

        def front_norm_a(ld, ci):
            xt, bxt = ld
            xn, bxn = xn_r.next()
            sm, bsm = sm_r.next()
            S.op("act", lambda e: e.activation(out=xn[:], in_=xt[:], func=AF.Square, accum_out=sm[:, 0:1]),
                 reads=[bxt], writes=[bxn, bsm])
            S.op("dve", lambda e: e.tensor_scalar(out=sm[:, 1:2], in0=sm[:, 0:1], scalar1=1.0 / D, scalar2=EPS, op0=ALU.mult, op1=ALU.add),
                 reads=[bsm], writes=[bsm])
            S.op("pool", lambda e: e.tensor_tensor(out=sm[:, 2:3], in0=sm[:, 1:2], in1=neghalf[:, 0:1], op=ALU.pow),
                 reads=[bsm, bneghalf], writes=[bsm])
            S.op("dve", lambda e: e.tensor_scalar(out=xn[:], in0=xt[:], scalar1=sm[:, 2:3], scalar2=None, op0=ALU.mult),
                 reads=[bxt, bsm], writes=[bxn])
            return (xt, bxt, xn, bxn, ci)

        def front_norm_b(na):
            xt, bxt, xn, bxn, ci = na

            def f_tp(e):
                last = None
                for k in range(8):
                    last = e.transpose(out=tps[:, k * P:(k + 1) * P], in_=xn[:, k * P:(k + 1) * P], identity=identb[:])
                return last
            S.op("pe", f_tp, reads=[bxn, bidentb], writes=[btps])
            hT, bhT = hT_r.next()

            def f_ev(e):
                last = None
                for k in range(8):
                    last = e.tensor_scalar(out=hT[:, k, :], in0=tps[:, k * P:(k + 1) * P],
                                           scalar1=gsC[:, k, ci:ci + 1], scalar2=adaC[:, k, ci:ci + 1],
                                           op0=ALU.mult, op1=ALU.add)
                return last
            S.op("dve", f_ev, reads=[btps, bgsC, badaC], writes=[bhT])
            return (xt, bxt, hT, bhT)

        def proj_fm(hT, bhT, col0, nchunks):
            pst, bps = psr.next()
            items = [(pst[:, c * P:(c + 1) * P],
                      [(win[:, k, col0 + c * P: col0 + (c + 1) * P], hT[:, k, :]) for k in range(8)])
                     for c in range(nchunks)]
            mm_group(pst, bps, items, [bhT] + bwin)
            return pst, bps

        def proj_tm(hT, bhT, col0, ncols):
            pst, bps = psr.next()
            mm_group(pst, bps, [(pst[:, 0:ncols], [(hT[:, k, :], win[:, k, col0:col0 + ncols]) for k in range(8)])],
                     [bhT] + bwin)
            return pst, bps

        def out_proj(l, u, t, colmajor, xt, bxt, mixT, bmix):
            base, T, ci, is_s = units[u]
            t4, bt4 = t4_r.next()
            for n in range(2):
                pst, bps = psr.next()
                mm_group(pst, bps, [(pst[:, :], [(mixT[:, k, :], wout[:, k, n * 512:(n + 1) * 512]) for k in range(8)])],
                         [bmix] + bwout)
                S.op("dve", lambda e, pst=pst, n=n: e.tensor_tensor(
                    out=t4[:, n * 512:(n + 1) * 512], in0=pst[:, :], in1=gateb[ci][:, n * 512:(n + 1) * 512], op=ALU.mult),
                    reads=[bps, bgateb[ci]], writes=[bt4])
                S.op("dve", lambda e, n=n: e.tensor_tensor(
                    out=t4[:, n * 512:(n + 1) * 512], in0=t4[:, n * 512:(n + 1) * 512], in1=xt[:, n * 512:(n + 1) * 512], op=ALU.add),
                    reads=[bt4, bxt], writes=[bt4])
            if l == depth - 1:
                yt, byt = t4_r.next()
                sm, bsm = sm_r.next()
                S.op("act", lambda e: e.activation(out=yt[:], in_=t4[:], func=AF.Square, accum_out=sm[:, 0:1]),
                     reads=[bt4], writes=[byt, bsm])
                S.op("dve", lambda e: e.tensor_scalar(out=sm[:, 1:2], in0=sm[:, 0:1], scalar1=1.0 / D, scalar2=EPS, op0=ALU.mult, op1=ALU.add),
                     reads=[bsm], writes=[bsm])
                S.op("pool", lambda e: e.tensor_tensor(out=sm[:, 2:3], in0=sm[:, 1:2], in1=neghalf[:, 0:1], op=ALU.pow),
                     reads=[bsm, bneghalf], writes=[bsm])
                S.op("dve", lambda e: e.scalar_tensor_tensor(out=yt[:], in0=t4[:], scalar=sm[:, 2:3], in1=fgb[:],
                                                             op0=ALU.mult, op1=ALU.mult),
                     reads=[bt4, bsm, bfgb], writes=[byt])
                dst, dbase = (ys_d, 0) if is_s else (yp_d, base)
                src_t, bsrc = yt, byt
                dwr = []
            else:
                dst, dbase = xscr[(l + 1) % 2], base
                src_t, bsrc = t4, bt4
                dwr = [bxs[(l + 1) % 2][u]]
            pcs = x_pieces(dst, dbase, t, colmajor)
            S.dma("sp", lambda e, pcs=pcs, src_t=src_t: [e.dma_start(out=ap, in_=src_t[lo:hi, :]) for ap, lo, hi in pcs],
                  reads=[bsrc], writes=dwr, n=len(pcs))

        def gla_prepA(l, d, hT, bhT):
            j = l // 2
            jd = j * 2 + d
            pst, bps = psr.next()
            lrc = LRF0 if d == 0 else LRB0
            mm_group(pst, bps, [(pst[0:16, 0:P], [(win[:, k, lrc:lrc + 16], hT[:, k, :]) for k in range(8)])], [bhT] + bwin)
            lrT, blrT = lrT_r.next()
            S.op("dve", lambda e, pst=pst: e.tensor_copy(out=lrT[0:16, :], in_=pst[0:16, 0:P]), reads=[bps], writes=[blrT])
            psv, bpsv = proj_tm(hT, bhT, V0, 512)
            vt, bvt = vt_r.next()
            S.op("dve", lambda e: e.tensor_copy(out=vt[:], in_=psv[:, :]), reads=[bpsv], writes=[bvt])
            psg, bpsg = psr.next()
            mm_group(psg, bpsg, [(psg[:, 0:256], [(lrT[:, :], wgk[:, jd, :])])], [blrT, bwgk])
            eg, beg = eg_r.next()
            S.op("act", lambda e: e.activation(out=eg[:], in_=psg[:, 0:256], func=AF.Exp, scale=-1.0), reads=[bpsg], writes=[beg])
            lg, blg = eg, beg
            S.op("act", lambda e: e.activation(out=lg[:], in_=eg[:], func=AF.Ln, bias=1.0), reads=[beg], writes=[blg])
            return dict(lg=lg, blg=blg, vt=vt, bvt=bvt)

        def gla_prepB(l, d, hT, bhT, pa, mid=None):
            last = P - 1 if d == 0 else 0
            lg, blg, vt, bvt = pa["lg"], pa["blg"], pa["vt"], pa["bvt"]
            psq, bpsq = proj_fm(hT, bhT, Q0, 4)
            psb, bpsb = psr.next()
            mm_group(psb, bpsb, [(psb[:, c * P:(c + 1) * P], [(lg[:, c * P:(c + 1) * P], tri[:, d, :])]) for c in range(2)],
                     [blg, btri])
            Eq, bEq = Eq_r.next()
            Ek, bEk = Ek_r.next()
            sm, bsm = sm_r.next()
            S.op("act", lambda e: e.activation(out=Eq[:], in_=psb[:, 0:256], func=AF.Exp, bias=math.log(0.125)), reads=[bpsb], writes=[bEq])
            S.op("act", lambda e: e.activation(out=Ek[:], in_=psb[:, 0:256], func=AF.Exp, scale=-1.0), reads=[bpsb], writes=[bEk])
            S.op("act", lambda e: e.activation(out=sm[:, 0:2], in_=psb[:, 0:256].rearrange("p (c n) -> p c n", c=2)[:, :, last],
                                               func=AF.Exp), reads=[bpsb], writes=[bsm])
            qd, bqd = qd_r.next()
            kd, bkd = kd_r.next()
            S.op("dve", lambda e: e.tensor_tensor(out=qd[:], in0=psq[:, 0:256], in1=Eq[:], op=ALU.mult), reads=[bpsq, bEq], writes=[bqd])
            S.op("dve", lambda e: e.tensor_tensor(out=kd[:], in0=psq[:, 256:512], in1=Ek[:], op=ALU.mult), reads=[bpsq, bEk], writes=[bkd])
            if mid is not None:
                mid()
            def f_tp(e):
                e.transpose(out=tps[:, 0:P], in_=kd[:, 0:P], identity=identb[:])
                return e.transpose(out=tps[:, P:2 * P], in_=kd[:, P:2 * P], identity=identb[:])
            S.op("pe", f_tp, reads=[bkd, bidentb], writes=[btps])
            kdt, bkdt = kdt_r.next()
            S.op("dve", lambda e: e.tensor_copy(out=kdt[:], in_=tps[:, 0:256]), reads=[btps], writes=[bkdt])
            psa0, bpsa0 = psr.next()
            psa1, bpsa1 = psr.next()
            psas = (psa0, psa1)
            items = []
            for h in (0, 2, 1, 3):
                lo = (h % 2) * 64
                cs = (h // 2) * P
                items.append((psas[h % 2][:, (h // 2) * P:(h // 2 + 1) * P], [(kd[lo:lo + 64, cs:cs + P], qd[lo:lo + 64, cs:cs + P])]))

            def fn_att(e, items=items):
                last = None
                for out_ap, pairs in items:
                    for l_, r_ in pairs:
                        last = e.matmul(out_ap, lhsT=l_, rhs=r_, start=True, stop=True)
                return last
            S.op("pe", fn_att, reads=[bkd, bqd], writes=[bpsa0, bpsa1])
            att, batt = att_r.next()
            attv = att[:].rearrange("p (a b) n -> p a b n", b=2)

            def fn_mask(e):
                e.tensor_tensor(out=attv[:, :, 0, :], in0=psa0[:, 0:256].rearrange("p (a n) -> p a n", a=2),
                                in1=mask[:, d, :].unsqueeze(1).to_broadcast([P, 2, P]), op=ALU.mult)
                return e.tensor_tensor(out=attv[:, :, 1, :], in0=psa1[:, 0:256].rearrange("p (a n) -> p a n", a=2),
                                       in1=mask[:, d, :].unsqueeze(1).to_broadcast([P, 2, P]), op=ALU.mult)
            S.op("dve", fn_mask, reads=[bpsa0, bpsa1, bmask], writes=[batt])
            return dict(qd=qd, bqd=bqd, vt=vt, bvt=bvt, att=att, batt=batt, kdt=kdt, bkdt=bkdt, sm=sm, bsm=bsm)

        def gla_scan(l, d, hd):
            qd, bqd, vt, bvt, att, batt = hd["qd"], hd["bqd"], hd["vt"], hd["bvt"], hd["att"], hd["batt"]
            kdt, bkdt, sm, bsm = hd["kdt"], hd["bkdt"], hd["sm"], hd["bsm"]
            pso, bpso = psr.next()
            items = []
            for h in range(4):
                lo = (h % 2) * 64
                cs = (h // 2) * P
                items.append((pso[:, h * P:(h + 1) * P],
                              [(Sbf[d][lo:lo + 64, h // 2, :], qd[lo:lo + 64, cs:cs + P]),
                               (vt[:, h * P:(h + 1) * P], att[:, h, :])]))
            mm_group(pso, bpso, items, [bSbf[d], bqd, bvt, batt])
            pss, bpss = psr.next()
            mm_group(pss, bpss, [(pss[:, p * 256:(p + 1) * 256], [(kdt[:, p * P:(p + 1) * P], vt[:, p * 256:(p + 1) * 256])])
                                 for p in range(2)], [bkdt, bvt])
            tS, btS = tS_r.next()
            pv = pss[:, :].rearrange("p (a n) -> p a n", a=2)

            def f_ts(e):
                e.tensor_tensor(out=tS[0:64, :, :], in0=pv[0:64, :, 0:P], in1=Sst[d][0:64, :, :], op=ALU.add)
                return e.tensor_tensor(out=tS[64:128, :, :], in0=pv[64:128, :, P:2 * P], in1=Sst[d][64:128, :, :], op=ALU.add)
            S.op("dve", f_ts, reads=[bpss, bSst[d]], writes=[btS])

            def f_s(e):
                e.tensor_scalar(out=Sst[d][:, 0, :], in0=tS[:, 0, :], scalar1=sm[:, 0:1], scalar2=None, op0=ALU.mult)
                return e.tensor_scalar(out=Sst[d][:, 1, :], in0=tS[:, 1, :], scalar1=sm[:, 1:2], scalar2=None, op0=ALU.mult)
            S.op("dve", f_s, reads=[btS, bsm], writes=[bSst[d]])

            def f_sb(e):
                e.activation(out=Sbf[d][:, 0, :], in_=tS[:, 0, :], func=AF.Identity, scale=sm[:, 0:1])
                return e.activation(out=Sbf[d][:, 1, :], in_=tS[:, 1, :], func=AF.Identity, scale=sm[:, 1:2])
            S.op("act", f_sb, reads=[btS, bsm], writes=[bSbf[d]])
            return pso, bpso

        def o_scr_ap(base, t):
            return oscr[:, base + t * P: base + (t + 1) * P].rearrange("(h v) n -> v h n", v=P)

        def state_ap(dram5, hh):
            return dram5.rearrange("(p hh) k v -> hh k p v", hh=2)[hh]

        def init_state(l, u, d):
            base, T, ci, is_s = units[u]
            j = l // 2
            for d in (d,):
                if is_s:
                    S.dma("sp", lambda e, d=d: [e.dma_start(out=Sst[d][hh * 64:(hh + 1) * 64, :, :], in_=state_ap(st0_d[j, d], hh))
                                                for hh in range(2)], writes=[bSst[d]], n=2)
                    S.op("act", lambda e, d=d: e.activation(out=Sbf[d][:], in_=Sst[d][:], func=AF.Copy), reads=[bSst[d]], writes=[bSbf[d]])
                else:
                    S.op("pool", lambda e, d=d: e.memset(Sst[d][:], 0.0), writes=[bSst[d]])
                    S.op("pool", lambda e, d=d: e.memset(Sbf[d][:], 0.0), writes=[bSbf[d]])

        def store_state(l, u, d):
            base, T, ci, is_s = units[u]
            if is_s:
                return
            j = l // 2
            S.dma("sp", lambda e: [e.dma_start(out=state_ap(ns_d[u, j, d], hh), in_=Sst[d][hh * 64:(hh + 1) * 64, :, :])
                                   for hh in range(2)], reads=[bSst[d]], n=2)

        def even_pass1_tile(l, u, t, hd, first=False, last=False):
            base = units[u][0]
            if first:
                init_state(l, u, 1)
            pso, bpso = gla_scan(l, 1, hd)
            if last:
                store_state(l, u, 1)
            ob, bob = f2k.next()
            S.op("dve", lambda e: e.tensor_copy(out=ob[:], in_=pso[:, :]), reads=[bpso], writes=[bob])
            S.dma("sp", lambda e: e.dma_start(out=o_scr_ap(base, t), in_=v4(ob[:])), reads=[bob], writes=[bos[u]])

        def even_pass2_tile(l, u, t, colmajor, fr, hd, first=False, last=False, pend=None):
            base, T, ci, is_s = units[u]
            j = l // 2
            xt, bxt, hT, bhT = fr
            orl, borl = f2k.next()
            S.dma("sp", lambda e: e.dma_start(out=v4(orl[:]), in_=o_scr_ap(base, t)), reads=[bos[u]], writes=[borl])
            mixT, bmix = mix_r.next()
            psvs, bpsvs = proj_tm(hT, bhT, VS0, 512)
            vsn, bvsn = vsn_r.next()
            sm, bsm = sm_r.next()
            S.op("act", lambda e: e.activation(out=vsn[:], in_=psvs[:, :], func=AF.Square, accum_out=sm[:, 0:1]),
                 reads=[bpsvs], writes=[bvsn, bsm])
            S.op("dve", lambda e: e.tensor_scalar(out=sm[:, 1:2], in0=sm[:, 0:1], scalar1=1.0 / 512, scalar2=EPS, op0=ALU.mult, op1=ALU.add),
                 reads=[bsm], writes=[bsm])
            S.op("pool", lambda e: e.tensor_tensor(out=sm[:, 2:3], in0=sm[:, 1:2], in1=neghalf[:, 0:1], op=ALU.pow),
                 reads=[bsm, bneghalf], writes=[bsm])
            S.op("dve", lambda e: e.scalar_tensor_tensor(out=vsn[:], in0=psvs[:, :], scalar=sm[:, 2:3], in1=sgugb[:, :],
                                                         op0=ALU.mult, op1=ALU.mult), reads=[bpsvs, bsm, bsgugb], writes=[bvsn])
            psga, bpsga = proj_fm(hT, bhT, GA0, 4)
            sga, bsga = f2k.next()
            S.op("act", lambda e: e.activation(out=sga[:], in_=psga[:, :], func=AF.Silu), reads=[bpsga], writes=[bsga])
            psu, bpsu = proj_fm(hT, bhT, U0, 4)
            usb, busb = f2k.next()
            S.op("act", lambda e: e.activation(out=usb[:], in_=psu[:, :], func=AF.Copy), reads=[bpsu], writes=[busb])
            psgb, bpsgb = proj_fm(hT, bhT, GB0, 4)
            sgb, bsgb = f2k.next()
            S.op("act", lambda e: e.activation(out=sgb[:], in_=psgb[:, :], func=AF.Silu), reads=[bpsgb], writes=[bsgb])
            pssp, bpssp = psr.next()
            mm_group(pssp, bpssp, [(pssp[:, h * P:(h + 1) * P], [(vsn[:, h * P:(h + 1) * P], wsT[:, j, h, :])]) for h in range(4)],
                     [bvsn, bwsT])
            t2, bt2 = f2k.next()
            S.op("dve", lambda e: e.tensor_tensor(out=t2[:], in0=pssp[:, :], in1=bsb[:, :], op=ALU.add), reads=[bpssp, bbsb], writes=[bt2])
            S.op("pool", lambda e: e.tensor_tensor(out=t2[:], in0=t2[:], in1=usb[:], op=ALU.mult), reads=[bt2, busb], writes=[bt2])
            S.op("pool", lambda e: e.tensor_tensor(out=mixT[:, 4:8, :], in0=v4(t2[:]), in1=v4(sgb[:]), op=ALU.mult),
                 reads=[bt2, bsgb], writes=[bmix])
            if pend is not None:
                out_proj(*pend)
            if first:
                init_state(l, u, 0)
            pso, bpso = gla_scan(l, 0, hd)
            if last:
                store_state(l, u, 0)
            ob, bob = f2k.next()
            S.op("dve", lambda e: e.tensor_tensor(out=ob[:], in0=pso[:, :], in1=orl[:], op=ALU.add), reads=[bpso, borl], writes=[bob])
            osqb, bosqb = osqb_r.next()
            S.op("act", lambda e: e.activation(out=osqb[:], in_=ob[:], func=AF.Square), reads=[bob], writes=[bosqb])

            def epilogue():
                osq, bosq = f2k.next()
                psss, bpsss = psr.next()
                mm_group(psss, bpsss, [(psss[:, :], [(ones128[:, :], osqb[:, :])])], [bones, bosqb])
                rso, brso = f2k.next()
                S.op("act", lambda e: e.activation(out=rso[:], in_=psss[:, :], func=AF.Ln, bias=EPS), reads=[bpsss], writes=[brso])
                S.op("act", lambda e: e.activation(out=rso[:], in_=rso[:], func=AF.Exp, scale=-0.5), reads=[brso], writes=[brso])
                S.op("dve", lambda e: e.scalar_tensor_tensor(out=osq[:], in0=ob[:], scalar=colB[:, GNG0 + j:GNG0 + j + 1], in1=rso[:],
                                                             op0=ALU.mult, op1=ALU.mult), reads=[bob, bcolB, brso], writes=[bosq])
                S.op("pool", lambda e: e.tensor_tensor(out=mixT[:, 0:4, :], in0=v4(osq[:]), in1=v4(sga[:]), op=ALU.mult),
                     reads=[bosq, bsga], writes=[bmix])
            return ((l, u, t, colmajor, xt, bxt, mixT, bmix), epilogue)

        def even_layer(l, ulist):
            for pas in (1, 2):
                d = 1 if pas == 1 else 0
                stream = []
                for u in ulist:
                    base, T, ci, is_s = units[u]
                    nu = T // P
                    order = list(range(nu - 1, -1, -1)) if pas == 1 else list(range(nu))
                    cm = is_s and (l // 2) % 2 == 1
                    for k, t in enumerate(order):
                        stream.append((u, t, cm, ci, k == 0, k == nu - 1))
                n = len(stream)

                def LD(k):
                    u, t, cm, ci, f, la = stream[k]
                    return front_load(l, u, t, cm)

                def NA(k, ld):
                    return front_norm_a(ld, stream[k][3])
                A = 4 if pas == 1 else 3
                lds = {k: LD(k) for k in range(min(A, n))}
                nas, frs, pas_, hds = {}, {}, {}, {}
                nas[0] = NA(0, lds[0])
                frs[0] = front_norm_b(nas.pop(0))
                pas_[0] = gla_prepA(l, d, frs[0][2], frs[0][3])
                if n > 1:
                    nas[1] = NA(1, lds[1])
                    frs[1] = front_norm_b(nas.pop(1))
                hds[0] = gla_prepB(l, d, frs[0][2], frs[0][3], pas_.pop(0))
                if n > 1:
                    pas_[1] = gla_prepA(l, d, frs[1][2], frs[1][3])
                if n > 2:
                    nas[2] = NA(2, lds[2])
                pend = None
                epi = None
                for i in range(n):
                    u, t, cm, ci, first, last = stream[i]
                    if i + A < n:
                        lds[i + A] = LD(i + A)
                    if i + 2 < n:
                        frs[i + 2] = front_norm_b(nas.pop(i + 2))
                    if pas == 1 and i + 3 < n:
                        nas[i + 3] = NA(i + 3, lds[i + 3])
                    def mid_fn(i=i, epi=epi):
                        if i + 2 < n:
                            pas_[i + 2] = gla_prepA(l, d, frs[i + 2][2], frs[i + 2][3])
                        if epi is not None:
                            epi()
                    if i + 1 < n:
                        hds[i + 1] = gla_prepB(l, d, frs[i + 1][2], frs[i + 1][3], pas_.pop(i + 1), mid=mid_fn)
                    else:
                        mid_fn()
                    epi = None
                    if pas == 2 and i + 3 < n:
                        nas[i + 3] = NA(i + 3, lds[i + 3])
                    if pas == 1:
                        frs.pop(i)
                        even_pass1_tile(l, u, t, hds.pop(i), first, last)
                    else:
                        pend, epi = even_pass2_tile(l, u, t, cm, frs.pop(i), hds.pop(i), first, last, pend)
                if epi is not None:
                    epi()
                if pend is not None:
                    out_proj(*pend)

        def odd_A1(l, u, t, n, fr, k):
            xt, bxt, hT, bhT = fr
            s = k % 2
            sp_ = 1 - s
            pxc, bpxc = proj_fm(hT, bhT, XC0, 4)
            S.op("act", lambda e: e.activation(out=XC[s][:, :, 8:136], in_=v4(pxc[:, :]), func=AF.Copy), reads=[bpxc], writes=[bXC[s]])
            if t == 0:
                S.op("pool", lambda e: e.memset(XC[s][:, :, 0:8], 0.0), writes=[bXC[s]])
            else:
                S.op("pool", lambda e: e.tensor_copy(out=XC[s][:, :, 0:8], in_=XC[sp_][:, :, 128:136]), reads=[bXC[sp_]], writes=[bXC[s]])
                S.op("pool", lambda e: e.tensor_copy(out=XC[sp_][:, :, 136:144], in_=XC[s][:, :, 8:16]), reads=[bXC[s]], writes=[bXC[sp_]])
            if t == n - 1:
                S.op("pool", lambda e: e.memset(XC[s][:, :, 136:144], 0.0), writes=[bXC[s]])

        def odd_A2(l, u, t, n, fr, held, k):
            xt, bxt, hT, bhT = fr
            s = k % 2
            sp_ = 1 - s
            pxd, bpxd = proj_fm(hT, bhT, XD0, 4)
            xd, bxd = f2k.next()
            S.op("act", lambda e: e.activation(out=xd[:], in_=pxd[:, :], func=AF.Copy), reads=[bpxd], writes=[bxd])
            pcd, bpcd = proj_fm(hT, bhT, CD0, 4)
            S.op("dve", lambda e: e.tensor_tensor(out=UP[s][:, :, 1:129], in0=v4(pcd[:, :]), in1=v4(xd[:]), op=ALU.mult),
                 reads=[bpcd, bxd], writes=[bUP[s]])
            if t == 0:
                S.op("pool", lambda e: e.memset(UP[s][:, :, 0:1], 0.0), writes=[bUP[s]])
            else:
                S.op("dve", lambda e: e.tensor_copy(out=UP[s][:, :, 0:1], in_=UP[sp_][:, :, 128:129]), reads=[bUP[sp_]], writes=[bUP[s]])
                S.op("dve", lambda e: e.tensor_copy(out=UP[sp_][:, :, 129:130], in_=UP[s][:, :, 1:2]), reads=[bUP[s]], writes=[bUP[sp_]])
            if t == n - 1:
                S.op("pool", lambda e: e.memset(UP[s][:, :, 129:130], 0.0), writes=[bUP[s]])
            pbd, bpbd = proj_fm(hT, bhT, BD0, 4)
            bd, bbd = hold_r.next()
            S.op("act", lambda e: e.activation(out=bd[:], in_=pbd[:, :], func=AF.Copy), reads=[bpbd], writes=[bbd])
            pgd, bpgd = proj_fm(hT, bhT, GD0, 4)
            sgd, bsgd = hold_r.next()
            S.op("act", lambda e: e.activation(out=sgd[:], in_=pgd[:, :], func=AF.Silu), reads=[bpgd], writes=[bsgd])
            pgc, bpgc = proj_fm(hT, bhT, GC0, 4)
            sgc, bsgc = hold_r.next()
            S.op("act", lambda e: e.activation(out=sgc[:], in_=pgc[:, :], func=AF.Silu), reads=[bpgc], writes=[bsgc])
            held[k] = (xt, bxt, sgc, bsgc, bd, bbd, sgd, bsgd)

        def odd_Bpool(l, u, t, n, k):
            s = k % 2
            X = XC[s]
            S.op("pool", lambda e: e.tensor_tensor(out=L1[:], in0=X[:, :, 1:144], in1=X[:, :, 0:143], op=ALU.add), reads=[bXC[s]], writes=[bL1])
            S.op("pool", lambda e: e.tensor_tensor(out=L2[:], in0=L1[:, 1:4, 0:141], in1=L1[:, 1:4, 2:143], op=ALU.add), reads=[bL1], writes=[bL2])
            S.op("pool", lambda e: e.tensor_tensor(out=L3[:], in0=L2[:, 1:3, 0:137], in1=L2[:, 1:3, 4:141], op=ALU.add), reads=[bL2], writes=[bL3])
            S.op("pool", lambda e: e.tensor_tensor(out=L4[:], in0=L3[:, 1, 0:128], in1=L3[:, 1, 8:136], op=ALU.add), reads=[bL3], writes=[bL4])
            var = 0 if t == 0 else (2 if t == n - 1 else 1)
            pooled, bpooled = pooled_r.next()
            srcs = (L1[:, 0, 7:135], L2[:, 0, 6:134], L3[:, 0, 4:132], L4[:, :])
            if var == 1:
                def f_pl(e):
                    last = None
                    for g, w in enumerate((2, 4, 8, 16)):
                        last = e.scalar_tensor_tensor(out=pooled[:, g, :], in0=srcs[g], scalar=1.0 / w, in1=X[:, g, 8:136],
                                                      op0=ALU.mult, op1=ALU.subtract)
                    return last
                S.op("dve", f_pl, reads=[bL1, bL2, bL3, bL4, bXC[s]], writes=[bpooled])
            else:
                pm, bpm = f2k.next()
                pmv = v4(pm[:])

                def f_pm(e):
                    last = None
                    for g in range(4):
                        last = e.tensor_tensor(out=pmv[:, g, :], in0=srcs[g], in1=rc[:, (0 if var == 0 else 1), g, :], op=ALU.mult)
                    return last
                S.op("dve", f_pm, reads=[bL1, bL2, bL3, bL4, brc], writes=[bpm])
                S.op("dve", lambda e: e.tensor_tensor(out=pooled[:], in0=pmv, in1=X[:, :, 8:136], op=ALU.subtract),
                     reads=[bpm, bXC[s]], writes=[bpooled])
            return (pooled, bpooled)

        def odd_Bconv(l, u, t, held, k):
            j = l // 2
            s = k % 2
            xt, bxt, sgc, bsgc, bd, bbd, sgd, bsgd = held[k]
            U = UP[s]
            y, by = f2k.next()
            yv = v4(y[:])

            def wc(k, c):
                o = WC0 + (j * 3 + k) * 4 + c
                return colB[:, o:o + 1]

            def f_c0(e):
                last = None
                for c in range(4):
                    last = e.tensor_scalar(out=yv[:, c, :], in0=U[:, c, 0:128], scalar1=wc(0, c), scalar2=None, op0=ALU.mult)
                return last
            S.op("dve", f_c0, reads=[bUP[s], bcolB], writes=[by])
            for k in (1, 2):
                def f_ck(e, k=k):
                    last = None
                    for c in range(4):
                        last = e.scalar_tensor_tensor(out=yv[:, c, :], in0=U[:, c, k:k + 128], scalar=wc(k, c), in1=yv[:, c, :],
                                                      op0=ALU.mult, op1=ALU.add)
                    return last
                S.op("dve", f_ck, reads=[bUP[s], bcolB, by], writes=[by])
            S.op("dve", lambda e: e.tensor_tensor(out=y[:], in0=y[:], in1=bd[:], op=ALU.mult), reads=[by, bbd], writes=[by])
            mixT, bmix = mix_r.next()
            S.op("dve", lambda e: e.tensor_tensor(out=mixT[:, 4:8, :], in0=yv, in1=v4(sgd[:]), op=ALU.mult),
                 reads=[by, bsgd], writes=[bmix])
            return (mixT, bmix)

        def odd_Bfin(l, u, t, colmajor, held, pl, mx, k):
            j = l // 2
            xt, bxt, sgc, bsgc, bd, bbd, sgd, bsgd = held.pop(k)
            pooled, bpooled = pl
            mixT, bmix = mx
            ppo, bppo = psr.next()
            mm_group(ppo, bppo, [(ppo[:, g * P:(g + 1) * P], [(wpool[:, j, g, :], pooled[:, g, :])]) for g in range(4)],
                     [bwpool, bpooled])

            def f_oc(e):
                last = None
                for g in range(4):
                    last = e.scalar_tensor_tensor(out=mixT[:, g, :], in0=ppo[:, g * P:(g + 1) * P],
                                                  scalar=colB[:, PS0 + j * 4 + g:PS0 + j * 4 + g + 1],
                                                  in1=sgc[:, g * P:(g + 1) * P], op0=ALU.mult, op1=ALU.mult)
                return last
            S.op("dve", f_oc, reads=[bppo, bcolB, bsgc], writes=[bmix])
            return (l, u, t, colmajor, xt, bxt, mixT, bmix)

        def odd_layer(l, ulist):
            stream = []
            for u in ulist:
                base, T, ci, is_s = units[u]
                nu = T // P
                cm = is_s and (l // 2) % 2 == 1
                for t in range(nu):
                    stream.append((u, t, nu, cm, ci))
            n = len(stream)
            held = {}

            def LD(k):
                u, t, nu, cm, ci = stream[k]
                return front_load(l, u, t, cm)

            def BP(k):
                u, t, nu, cm, ci = stream[k]
                return odd_Bpool(l, u, t, nu, k)

            def BC(k, pl):
                u, t, nu, cm, ci = stream[k]
                mx = odd_Bconv(l, u, t, held, k)
                return odd_Bfin(l, u, t, cm, held, pl, mx, k)
            lds = {0: LD(0)}
            if n > 1:
                lds[1] = LD(1)
            nas = {0: front_norm_a(lds[0], stream[0][4])}
            frs = {0: front_norm_b(nas.pop(0))}
            if n > 1:
                nas[1] = front_norm_a(lds[1], stream[1][4])
            pend = None
            for k in range(n):
                u, t, nu, cm, ci = stream[k]
                if k + 2 < n:
                    lds[k + 2] = LD(k + 2)
                if k + 1 < n:
                    frs[k + 1] = front_norm_b(nas.pop(k + 1))
                fr = frs.pop(k)
                odd_A1(l, u, t, nu, fr, k)
                if k + 2 < n:
                    nas[k + 2] = front_norm_a(lds[k + 2], stream[k + 2][4])
                if pend is not None:
                    out_proj(*pend)
                    pend = None
                pl = BP(k - 1) if k >= 1 else None
                odd_A2(l, u, t, nu, fr, held, k)
                if k >= 1:
                    pend = BC(k - 1, pl)
            if pend is not None:
                out_proj(*pend)
            pl = BP(n - 1)
            out_proj(*BC(n - 1, pl))

        for l in range(depth):
            load_weights(l)
            emit_adaln(l)
            ulist = list(usel if usel is not None else range(len(units)))
            if l % 2 == 0:
                even_layer(l, ulist)
            else:
                odd_layer(l, ulist)
        S.finish()
        S.emit(nc, st)
    return nc


def _constants():
    jj, ii = np.meshgrid(np.arange(P), np.arange(P), indexing="ij")
    le = (jj <= ii).astype(np.float32)
    ge = (jj >= ii).astype(np.float32)
    tri = np.stack([le * (-1.0 / 16.0), ge * (-1.0 / 16.0)]).astype(np.float32)
    mask = np.stack([le, ge]).astype(np.float32)
    rc = np.zeros((2, 4, P), np.float32)
    pos = np.arange(P)
    for g, w in enumerate((2, 4, 8, 16)):
        a = w // 2
        rc[0, g] = 1.0 / (np.minimum(pos, a) + a)
        rc[1, g] = 1.0 / (a + np.minimum(a, P - pos))
    sel = np.zeros((2, 2, P), np.float32)
    sel[0, 0, :] = 1.0
    sel[1, 1, :] = 1.0
    return dict(c_ident=np.eye(P, dtype=np.float32), c_tri=tri, c_mask=mask, c_rc=rc.reshape(-1), c_sel=sel,
                c_ones=np.ones((1, P), np.float32))


_NC_CACHE = {}


def make_in_maps(inputs, n_cores=8):
    f = lambda a: np.ascontiguousarray(np.asarray(a, dtype=np.float32))
    consts = _constants()
    shared = {k: f(inputs[k]) for k in ("w_ada", "b_ada", "norm_g", "w_in_even", "w_in_odd", "w_out", "w_gk", "b_gk",
                                        "gla_norm_g", "sgu_norm_g", "w_s", "b_s", "w_pool", "pool_scale", "w_conv",
                                        "final_norm_g")}
    xp = f(inputs["x_prompt"])
    xs = f(inputs["x_sample"])
    c = f(inputs["c"])
    cctx = f(inputs["c_ctx"])
    stg = f(inputs["state_gla"])
    maps = []
    for core in range(n_cores):
        b = (core // 2) % xs.shape[0]
        m = dict(shared)
        m.update(consts)
        m["xp"] = np.ascontiguousarray(xp[core * NPROMPT:(core + 1) * NPROMPT].reshape(NPROMPT * TP, D))
        m["xs"] = np.ascontiguousarray(xs[b])
        m["cond"] = np.ascontiguousarray(np.stack([cctx, c[b]]).reshape(16, P))
        m["st0"] = np.ascontiguousarray(stg[b])
        maps.append(m)
    return maps


def kernel(x_prompt, x_sample, c, state_gla, c_ctx, w_ada, b_ada, norm_g, w_in_even, w_in_odd, w_out, w_gk, b_gk,
           gla_norm_g, sgu_norm_g, w_s, b_s, w_pool, pool_scale, w_conv, final_norm_g):
    inputs = dict(x_prompt=x_prompt, x_sample=x_sample, c=c, state_gla=state_gla, c_ctx=c_ctx, w_ada=w_ada, b_ada=b_ada,
                  norm_g=norm_g, w_in_even=w_in_even, w_in_odd=w_in_odd, w_out=w_out, w_gk=w_gk, b_gk=b_gk,
                  gla_norm_g=gla_norm_g, sgu_norm_g=sgu_norm_g, w_s=w_s, b_s=b_s, w_pool=w_pool, pool_scale=pool_scale,
                  w_conv=w_conv, final_norm_g=final_norm_g)
    n_cores = 8
    maps = make_in_maps(inputs, n_cores)
    if "nc" not in _NC_CACHE:
        _NC_CACHE["nc"] = build_nc(4)
    res = run_bass_kernel_spmd(_NC_CACHE["nc"], maps, core_ids=list(range(n_cores)))
    r = res.results
    y_prompt = np.concatenate([r[i]["yp"].reshape(NPROMPT, TP, D) for i in range(n_cores)], axis=0).astype(np.float32)
    y_sample = np.stack([r[2 * b]["ys"] for b in range(4)], axis=0).astype(np.float32)
    new_state = np.concatenate([r[i]["ns"] for i in range(n_cores)], axis=0).astype(np.float32)
    return (y_prompt, y_sample, new_state)
```
